# Optimizing a Trainium2 kernel written in Bass

```python
import math
import jax, jax.numpy as jnp
from jax import lax
import numpy as np

D_MODEL = 1024
BATCH = 16
SEQ = 256
DEPTH = 4
DEC_BATCH = 2
DEC_SEQ = 2048
PAST_LEN = 512

GRID_W = 64
N_MIXERS = 3
N_LRU_LAYERS = (DEPTH + 2) // 3
N_SC_LAYERS = (DEPTH + 1) // 3
N_POOL_LAYERS = DEPTH // 3
D_RNN = D_MODEL
N_LRU_BLOCKS = 16
LRU_BLOCK = D_RNN // N_LRU_BLOCKS
LRU_CONV_W = 4
LRU_PAD_L = 2
LRU_PAD_R = 1
LRU_C = 8.0
SC_CONV_W = 3
POOL_WINDOWS = (2, 4, 8, 16)
N_POOL_GROUPS = 4
POOL_GROUP = D_MODEL // N_POOL_GROUPS
D_FF = 4 * D_MODEL
ALPHA = (2.0 * DEPTH) ** 0.25
BETA = (8.0 * DEPTH) ** -0.25
LN_EPS = 1e-5

kernel_name = 'hybrid_rglru_shortconv_pool_diffusion_step'


def layer_norm(x, g, b):
    xf = x.astype(jnp.float32)
    mu = jnp.mean(xf, axis=-1, keepdims=True)
    var = jnp.mean(jnp.square(xf - mu), axis=-1, keepdims=True)
    y = (xf - mu) * lax.rsqrt(var + LN_EPS)
    return (y * g.astype(jnp.float32) + b.astype(jnp.float32)).astype(x.dtype)


def depthwise_conv(x, w, pad_left, pad_right):
    return lax.conv_general_dilated(
        x, w[:, None, :].astype(x.dtype), window_strides=(1,),
        padding=[(pad_left, pad_right)],
        dimension_numbers=('NWC', 'WIO', 'NWC'),
        feature_group_count=x.shape[-1])


def linear_recurrence(a, u, h0):
    def combine(l, r):
        return l[0] * r[0], r[0] * l[1] + r[1]
    a_cum, u_cum = lax.associative_scan(combine, (a, u), axis=1)
    h = a_cum * h0[:, None, :] + u_cum
    return h, h[:, -1]


def rglru_direction(xc, w_a, b_a, w_x, b_x, lam, h0, reverse):
    bsz, T, _ = xc.shape
    xb = xc.reshape(bsz, T, N_LRU_BLOCKS, LRU_BLOCK)
    r = jax.nn.sigmoid(jnp.einsum('btnd,nde->btne', xb, w_a).reshape(bsz, T, D_RNN) + b_a)
    ig = jax.nn.sigmoid(jnp.einsum('btnd,nde->btne', xb, w_x).reshape(bsz, T, D_RNN) + b_x)
    log_a = (-LRU_C * r.astype(jnp.float32)) * jax.nn.softplus(-lam.astype(jnp.float32))
    a = jnp.exp(log_a)
    u = jnp.sqrt(-jnp.expm1(2.0 * log_a)) * (ig * xc).astype(jnp.float32)
    if reverse:
        a, u = a[:, ::-1], u[:, ::-1]
    h, h_last = linear_recurrence(a, u, h0.astype(jnp.float32))
    if reverse:
        h = h[:, ::-1]
    return h, h_last


def lru_mixer(x, w_in, conv_w, conv_b, w_a, b_a, w_x, b_x, lam, w_out, h0):
    gate, rec = jnp.split(x @ w_in, 2, axis=-1)
    xc = depthwise_conv(rec, conv_w, LRU_PAD_L, LRU_PAD_R) + conv_b
    h_f, s_f = rglru_direction(xc, w_a[0], b_a[0], w_x[0], b_x[0], lam[0], h0[:, 0], False)
    h_b, s_b = rglru_direction(xc, w_a[1], b_a[1], w_x[1], b_x[1], lam[1], h0[:, 1], True)
    y = jax.nn.gelu(gate) * (h_f + h_b).astype(x.dtype)
    return y @ w_out, jnp.stack([s_f, s_b], axis=1).astype(x.dtype)


def short_conv_mixer(x, w_in, conv_w, w_out):
    b_gate, c_gate, v = jnp.split(x @ w_in, 3, axis=-1)
    y = b_gate * depthwise_conv(c_gate * v, conv_w, 1, 1)
    return y @ w_out


def pool_mixer(x, w, scale):
    bsz, T, _ = x.shape
    xf = x.astype(jnp.float32)
    S = jnp.concatenate([jnp.zeros((bsz, 1, D_MODEL), jnp.float32), jnp.cumsum(xf, axis=1)], axis=1)
    t = jnp.arange(T)
    pooled = []
    for g, win in enumerate(POOL_WINDOWS):
        lo = jnp.clip(t - win // 2, 0, T)
        hi = jnp.clip(t + win - win // 2, 0, T)
        Sg = S[..., g * POOL_GROUP:(g + 1) * POOL_GROUP]
        s = jnp.take(Sg, hi, axis=1) - jnp.take(Sg, lo, axis=1)
        cnt = (hi - lo).astype(jnp.float32)
        pooled.append(s / cnt[None, :, None])
    d = (jnp.concatenate(pooled, axis=-1) - xf).astype(x.dtype)
    y = jnp.einsum('btgc,gcd->btgd', d.reshape(bsz, T, N_POOL_GROUPS, POOL_GROUP), w)
    return y.reshape(bsz, T, D_MODEL) * scale


def squared_relu_mlp(x, w1, b1, w2, b2):
    return jnp.square(jax.nn.relu(x @ w1 + b1)) @ w2 + b2


def grid_pos_embed(T):
    rows = T // GRID_W
    row = jnp.broadcast_to(jnp.arange(rows)[:, None], (rows, GRID_W)).reshape(-1).astype(jnp.float32)
    col = jnp.broadcast_to(jnp.arange(GRID_W)[None, :], (rows, GRID_W)).reshape(-1).astype(jnp.float32)
    nf = D_MODEL // 4
    freq = jnp.exp(-math.log(10000.0) * jnp.arange(nf, dtype=jnp.float32) / nf)
    ang_r = row[:, None] * freq
    ang_c = col[:, None] * freq
    return jnp.concatenate([jnp.sin(ang_r), jnp.cos(ang_r), jnp.sin(ang_c), jnp.cos(ang_c)], axis=-1)


def trunk(x, cond, h0_all, p):
    states = []
    for i in range(DEPTH):
        mod = jax.nn.silu(cond) @ p['w_mod'][i] + p['b_mod'][i]
        sh1, sc1, g1, sh2, sc2, g2 = jnp.split(mod[:, None, :], 6, axis=-1)
        h = x * (1 + sc1) + sh1
        kind, j = i % N_MIXERS, i // N_MIXERS
        if kind == 0:
            out, st = lru_mixer(h, p['lru_w_in'][j], p['lru_conv_w'][j], p['lru_conv_b'][j],
                                p['lru_w_a'][j], p['lru_b_a'][j], p['lru_w_x'][j], p['lru_b_x'][j],
                                p['lru_lambda'][j], p['lru_w_out'][j], h0_all[:, j])
            states.append(st)
        elif kind == 1:
            out = short_conv_mixer(h, p['sc_w_in'][j], p['sc_conv_w'][j], p['sc_w_out'][j])
        else:
            out = pool_mixer(h, p['pool_w'][j], p['pool_scale'][j])
        x = layer_norm(ALPHA * x + g1 * out, p['ln_g'][i, 0], p['ln_b'][i, 0])
        h = x * (1 + sc2) + sh2
        f = squared_relu_mlp(h, p['mlp_w1'][i], p['mlp_b1'][i], p['mlp_w2'][i], p['mlp_b2'][i])
        x = layer_norm(ALPHA * x + g2 * f, p['ln_g'][i, 1], p['ln_b'][i, 1])
    return x, jnp.stack(states, axis=1)


def setup_inputs(seed: int = 0) -> dict:
    key = jax.random.key(seed)
    ks = jax.random.split(key, 32)
    f32 = jnp.float32
    def nrm(k, shape, s):
        return jax.random.normal(k, shape, f32) * s
    u = jax.random.uniform(ks[20], (N_LRU_LAYERS, 2, D_RNN), f32, minval=0.9, maxval=0.999)
    sig = u ** (1.0 / LRU_C)
    lam = jnp.log(sig) - jnp.log1p(-sig)
    return {
        'x_prompt': nrm(ks[0], (BATCH, SEQ, D_MODEL), 1.0),
        'x_sample': nrm(ks[1], (DEC_BATCH, DEC_SEQ, D_MODEL), 1.0),
        'state_lru': nrm(ks[2], (DEC_BATCH, N_LRU_LAYERS, 2, D_RNN), 0.5),
        'c': nrm(ks[3], (DEC_BATCH, D_MODEL), 1.0),
        'c_ctx': nrm(ks[4], (D_MODEL,), 1.0),
        'w_mod': nrm(ks[5], (DEPTH, D_MODEL, 6 * D_MODEL), D_MODEL ** -0.5),
        'b_mod': nrm(ks[6], (DEPTH, 6 * D_MODEL), 0.02),
        'ln_g': 1.0 + nrm(ks[7], (DEPTH, 2, D_MODEL), 0.02),
        'ln_b': nrm(ks[8], (DEPTH, 2, D_MODEL), 0.02),
        'mlp_w1': nrm(ks[9], (DEPTH, D_MODEL, D_FF), D_MODEL ** -0.5),
        'mlp_b1': nrm(ks[10], (DEPTH, D_FF), 0.02),
        'mlp_w2': nrm(ks[11], (DEPTH, D_FF, D_MODEL), BETA * D_FF ** -0.5),
        'mlp_b2': nrm(ks[12], (DEPTH, D_MODEL), 0.02),
        'lru_w_in': nrm(ks[13], (N_LRU_LAYERS, D_MODEL, 2 * D_RNN), D_MODEL ** -0.5),
        'lru_conv_w': nrm(ks[14], (N_LRU_LAYERS, LRU_CONV_W, D_RNN), LRU_CONV_W ** -0.5),
        'lru_conv_b': nrm(ks[15], (N_LRU_LAYERS, D_RNN), 0.02),
        'lru_w_a': nrm(ks[16], (N_LRU_LAYERS, 2, N_LRU_BLOCKS, LRU_BLOCK, LRU_BLOCK), LRU_BLOCK ** -0.5),
        'lru_b_a': nrm(ks[17], (N_LRU_LAYERS, 2, D_RNN), 0.02),
        'lru_w_x': nrm(ks[18], (N_LRU_LAYERS, 2, N_LRU_BLOCKS, LRU_BLOCK, LRU_BLOCK), LRU_BLOCK ** -0.5),
        'lru_b_x': nrm(ks[19], (N_LRU_LAYERS, 2, D_RNN), 0.02),
        'lru_lambda': lam,
        'lru_w_out': nrm(ks[21], (N_LRU_LAYERS, D_RNN, D_MODEL), BETA * D_RNN ** -0.5),
        'sc_w_in': nrm(ks[22], (N_SC_LAYERS, D_MODEL, 3 * D_MODEL), D_MODEL ** -0.5),
        'sc_conv_w': nrm(ks[23], (N_SC_LAYERS, SC_CONV_W, D_MODEL), SC_CONV_W ** -0.5),
        'sc_w_out': nrm(ks[24], (N_SC_LAYERS, D_MODEL, D_MODEL), BETA * D_MODEL ** -0.5),
        'pool_w': nrm(ks[25], (N_POOL_LAYERS, N_POOL_GROUPS, POOL_GROUP, POOL_GROUP), BETA * POOL_GROUP ** -0.5),
        'pool_scale': 1.0 + nrm(ks[26], (N_POOL_LAYERS, D_MODEL), 0.1),
    }


def reference(x_prompt, x_sample, state_lru, c, c_ctx, w_mod, b_mod, ln_g, ln_b,
              mlp_w1, mlp_b1, mlp_w2, mlp_b2,
              lru_w_in, lru_conv_w, lru_conv_b, lru_w_a, lru_b_a, lru_w_x, lru_b_x,
              lru_lambda, lru_w_out, sc_w_in, sc_conv_w, sc_w_out, pool_w, pool_scale):
    p = {
        'w_mod': w_mod, 'b_mod': b_mod, 'ln_g': ln_g, 'ln_b': ln_b,
        'mlp_w1': mlp_w1, 'mlp_b1': mlp_b1, 'mlp_w2': mlp_w2, 'mlp_b2': mlp_b2,
        'lru_w_in': lru_w_in, 'lru_conv_w': lru_conv_w, 'lru_conv_b': lru_conv_b,
        'lru_w_a': lru_w_a, 'lru_b_a': lru_b_a, 'lru_w_x': lru_w_x, 'lru_b_x': lru_b_x,
        'lru_lambda': lru_lambda, 'lru_w_out': lru_w_out,
        'sc_w_in': sc_w_in, 'sc_conv_w': sc_conv_w, 'sc_w_out': sc_w_out,
        'pool_w': pool_w, 'pool_scale': pool_scale,
    }
    h0_ctx = jnp.zeros((x_prompt.shape[0], N_LRU_LAYERS, 2, D_RNN), x_prompt.dtype)
    y_prompt, new_state_lru = trunk(x_prompt, c_ctx[None, :], h0_ctx, p)
    xs = x_sample + grid_pos_embed(x_sample.shape[1]).astype(x_sample.dtype)[None]
    y_sample, _ = trunk(xs, c, state_lru, p)
    return (y_prompt, y_sample, new_state_lru)
```

```python
import math
from collections import deque
from contextlib import ExitStack

import numpy as np
import concourse.bass as bass
import concourse.mybir as mybir
from concourse.bass_utils import run_bass_kernel_spmd

F32 = mybir.dt.float32
BF16 = mybir.dt.bfloat16
I32 = mybir.dt.int32
ALU = mybir.AluOpType
AF = mybir.ActivationFunctionType

P = 128
D = 1024
C = 8
T = 2048
NBLK = 4
TB = 512
NS = 8
L = 256
DFF = 4096
DEPTH = 4
ALPHA = (2.0 * DEPTH) ** 0.25
LN_EPS = 1e-5
LRU_C = 8.0
NCORES = 8
POOL_WINDOWS = (2, 4, 8, 16)
GELU_K = 2.0 * math.sqrt(2.0 / math.pi)
LRU_SKEW = True
ZB_ON_DVE = True
FQ0_BLK = False
OUTPROJ_BLK = False
LN_DRIP_OUT = 5
LN_DRIP_W2 = 3
FUSE_A = False
FUSE_A_L0 = False

SAMPLE_CORES = {0: 0, 1: 1}
PROMPT_CORES = {2: [0, 1, 2], 3: [3, 4, 5], 4: [6, 7, 8], 5: [9, 10, 11], 6: [12, 13], 7: [14, 15]}

VL = {}
_cur = 0


def _add(name, n):
    global _cur
    VL[name] = (_cur, n)
    _cur += n


_add('bmod', 4 * 48)
_add('lng', 4 * 2 * 8)
_add('lnb', 4 * 2 * 8)
_add('b1', 4 * 32)
_add('b2', 4 * 8)
_add('lru_cw', 2 * 4 * 8)
_add('lru_cb', 2 * 8)
_add('lru_ba', 2 * 2 * 8)
_add('lru_bx', 2 * 2 * 8)
_add('lru_lam', 2 * 2 * 8)
_add('sc_cw', 3 * 8)
_add('pool_sc', 8)
_add('h0', 2 * 2 * 8)
_add('m', 1)
_add('jf', 2)
_add('posr', 32)
_add('posc', 64)
_add('poolF', 4 * 2 * 8)
NV = _cur


def fm(v):
    v = np.asarray(v, np.float32)
    lead = v.shape[:-1]
    n = v.shape[-1] // P
    return np.moveaxis(v.reshape(*lead, n, P), -1, 0)


def weight_plan(n_layers=DEPTH):
    plan = []
    for l in range(n_layers):
        kind, j = l % 3, l // 3
        if kind == 0:
            plan.append(('bd', j))
            fused = FUSE_A and (l > 0 or FUSE_A_L0)
            for c0 in ((0, 1024, 512, 1536) if fused else (0, 512, 1024, 1536)):
                plan.append(('k8', 'lru_w_in', j, 0, c0))
            for c0 in (0, 512):
                plan.append(('k8', 'lru_w_out', j, 0, c0))
        elif kind == 1:
            for c in range(C):
                plan.append(('sc', c))
            for c0 in (0, 512):
                plan.append(('k8', 'sc_w_out', j, 0, c0))
        else:
            plan.append(('pool', j))
        for fq in range(4):
            for hf in range(2):
                plan.append(('k8', 'mlp_w1', l, 0, fq * 1024 + hf * 512))
            for mq in range(2):
                plan.append(('k8', 'mlp_w2', l, fq * 1024, mq * 512))
    return plan


def pack_group(g, inp):
    out = np.zeros((P, 4096), np.float32)
    if g[0] == 'k8':
        _, name, idx, r0, c0 = g
        W = inp[name][idx][r0:r0 + 1024, c0:c0 + 512]
        out[:, :] = W.reshape(8, P, 512).transpose(1, 0, 2).reshape(P, 4096)
    elif g[0] == 'sc':
        c = g[1]
        W = inp['sc_w_in'][0]
        W3 = np.concatenate([W[:, k * 1024 + c * P:k * 1024 + (c + 1) * P] for k in range(3)], axis=1)
        out[:, :3072] = W3.reshape(8, P, 384).transpose(1, 0, 2).reshape(P, 3072)
    elif g[0] == 'bd':
        j = g[1]
        t = 0
        for d in range(2):
            for nm in ('lru_w_a', 'lru_w_x'):
                w = inp[nm][j, d]
                for c in range(C):
                    blk = np.zeros((P, P), np.float32)
                    blk[0:64, 0:64] = w[2 * c]
                    blk[64:128, 64:128] = w[2 * c + 1]
                    out[:, (t * 8 + c) * P:(t * 8 + c + 1) * P] = blk
                t += 1
    elif g[0] == 'pool':
        w = inp['pool_w'][g[1]]
        for gg in range(4):
            for kc in range(2):
                out[:, (gg * 2 + kc) * 256:(gg * 2 + kc + 1) * 256] = w[gg, kc * P:(kc + 1) * P, :]
    return out


ENGS = ['pe', 'act', 'dve', 'pool', 'sp']
GR = 512


class Sched:
    def __init__(self):
        self.prog = {e: [] for e in ENGS}
        self.cnt = {e: 0 for e in ENGS}
        self.known = {e: {} for e in ENGS}
        self.gran = {}
        self.dmacnt = {}
        self.snap = {}

    def _keys(self, acc):
        if acc[0] == 'ps':
            return [('ps', acc[1])]
        _, lo, hi = acc
        return [('sb', k) for k in range(lo // GR, (hi - 1) // GR + 1)]

    def _collect(self, reads, writes):
        need = {}

        def upd(tok):
            if tok is not None and need.get(tok[0], 0) < tok[1]:
                need[tok[0]] = tok[1]

        for a in reads:
            for k in self._keys(a):
                g = self.gran.get(k)
                if g:
                    upd(g[0])
        for a in writes:
            for k in self._keys(a):
                g = self.gran.get(k)
                if g:
                    upd(g[0])
                    for s, v in g[1].items():
                        upd((s, v))
        return need

    def _waits(self, eng, need):
        kn = self.known[eng]
        for s, v in sorted(need.items(), key=lambda sv: -sv[1]):
            if kn.get(s, 0) < v:
                self.prog[eng].append(('w', s, v))
                kn[s] = v
                sn = self.snap.get((s, v))
                if sn:
                    for s2, v2 in sn.items():
                        if kn.get(s2, 0) < v2:
                            kn[s2] = v2

    def _record(self, tok, reads, writes):
        for a in reads:
            for k in self._keys(a):
                g = self.gran.setdefault(k, [None, {}])
                if g[1].get(tok[0], 0) < tok[1]:
                    g[1][tok[0]] = tok[1]
        for a in writes:
            for k in self._keys(a):
                self.gran[k] = [tok, {}]

    def op(self, eng, fn, reads=(), writes=()):
        self._waits(eng, self._collect(reads, writes))
        self.cnt[eng] += 1
        tok = (eng, self.cnt[eng])
        self.snap[tok] = dict(self.known[eng])
        self.prog[eng].append(('o', fn))
        self._record(tok, reads, writes)

    def dma(self, q, fn, sem, reads=(), writes=()):
        self._waits(q, self._collect(reads, writes))
        self.dmacnt[sem] = self.dmacnt.get(sem, 0) + 1
        tok = ('d:' + sem, 16 * self.dmacnt[sem])
        self.snap[tok] = dict(self.known[q])
        self.prog[q].append(('d', fn, 'd:' + sem))
        self._record(tok, reads, writes)
        return tok

    def wait(self, eng, tok):
        self._waits(eng, {tok[0]: tok[1]})


class Buf:
    def __init__(self, arena, lo, nbytes, dt):
        assert lo % 4 == 0 and nbytes % 4 == 0
        self.lo, self.nb, self.dt = lo, nbytes, dt
        self.es = 4 if dt in (F32, I32) else 2
        a = arena[:, lo // 4:(lo + nbytes) // 4]
        self.ap = a if dt == F32 else a.bitcast(dt)

    def acc(self, a=None, b=None):
        if a is None:
            return ('sb', self.lo, self.lo + self.nb)
        return ('sb', self.lo + a * self.es, self.lo + b * self.es)


def build(n_layers=DEPTH):
    nc = bass.Bass("TRN2", target_bir_lowering=False)
    plan = weight_plan(n_layers)
    NG = len(plan)

    xT = nc.dram_tensor("xT", [D, T], F32, kind="ExternalInput").ap()
    cvec = nc.dram_tensor("cvec", [P, D], F32, kind="ExternalInput").ap()
    vecs = nc.dram_tensor("vecs", [P, NV], F32, kind="ExternalInput").ap()
    wmodT = nc.dram_tensor("wmodT", [4 * 6144, D], F32, kind="ExternalInput").ap()
    wstream = nc.dram_tensor("wstream", [NG * P, 4096], F32, kind="ExternalInput").ap()
    onesm_d = nc.dram_tensor("onesm", [P, P], F32, kind="ExternalInput").ap()
    yT = nc.dram_tensor("yT", [D, T], F32, kind="ExternalOutput").ap()
    st_d = nc.dram_tensor("st", [P, 256], F32, kind="ExternalOutput").ap()
    xsp = nc.dram_tensor("xsp", [D, T], F32).ap()

    S = Sched()
    es = ExitStack()
    ARENA_F = 53184
    arena_t = es.enter_context(nc.sbuf_tensor("arena", [P, ARENA_F], F32))
    arena = arena_t[:]
    psb = [es.enter_context(nc.psum_tensor(f"ps{k}", [P, 512], F32))[:] for k in range(8)]

    X_LO, H_LO, R_LO, RING_LO, EXT_LO, SM_LO = 0, 65536, 98304, 163840, 188416, 196736
    RU_LO = R_LO + 32768
    X = [Buf(arena, X_LO + c * 8192, 8192, F32) for c in range(C)]
    H = [Buf(arena, H_LO + c * 4096, 4096, BF16) for c in range(C)]
    Y = [Buf(arena, R_LO + c * 4096, 4096, BF16) for c in range(C)]
    HID = Y
    RING = [Buf(arena, RING_LO + s * 8192, 8192, BF16) for s in range(3)]
    EXT = Buf(arena, EXT_LO, 8320, F32)
    sm_cur = [SM_LO]

    def small(nfloats, dt=F32, align=4):
        sm_cur[0] = (sm_cur[0] + align - 1) // align * align
        b = Buf(arena, sm_cur[0], nfloats * 4, dt)
        sm_cur[0] += nfloats * 4
        assert sm_cur[0] <= ARENA_F * 4
        return b

    MODL = [small(128, align=512) for _ in range(4)]
    DERL = [small(128, align=512) for _ in range(4)]
    VEC = small(NV, align=512)
    LRD = small(2 * 64)
    FRQ = small(2)
    FM = small(72)
    STB = small(256)
    ONESM = small(64, BF16)
    XCB1 = small(1024, BF16, align=512)

    def V(name, off=0, n=1):
        o = VL[name][0] + off
        return VEC.ap[:, o:o + n]

    def modcol(l, kind, c=0, n=1):
        o = kind * 8 + c
        return MODL[l].ap[:, o:o + n]

    def LA(l):
        return (MODL[l].acc(), DERL[l].acc())

    DN = {'op1': 0, 'op2': 1, 'g2b2': 2, 'ga1': 3, 'gb1': 4, 'xg1': 5, 'xb1': 6, 'z1s': 7,
          'ga2': 8, 'gb2': 9, 'xg2': 10, 'xb2': 11, 'hps': 12, 'tmp': 13}

    def der(l, name, c=0, n=1):
        o = DN[name] * 8 + c
        return DERL[l].ap[:, o:o + n]

    def act(out, in_, func, bias=0.0, scale=1.0, r=(), w=()):
        S.op('act', lambda e: e.activation(out=out, in_=in_, func=func, bias=bias, scale=scale), r, w)

    def tt(out, in0, in1, op, r=(), w=(), eng='dve'):
        S.op(eng, lambda e: e.tensor_tensor(out=out, in0=in0, in1=in1, op=op), r, w)

    def ts(out, in0, s1, s2, op0, op1=None, r=(), w=(), eng='dve'):
        if op1 is None:
            S.op(eng, lambda e: e.tensor_scalar(out=out, in0=in0, scalar1=s1, scalar2=None, op0=op0), r, w)
        else:
            S.op(eng, lambda e: e.tensor_scalar(out=out, in0=in0, scalar1=s1, scalar2=s2, op0=op0, op1=op1), r, w)

    def stt(out, in0, scalar, in1, op0, op1, r=(), w=(), accum=None):
        if accum is None:
            S.op('dve', lambda e: e.scalar_tensor_tensor(out=out, in0=in0, scalar=scalar, in1=in1, op0=op0, op1=op1), r, w)
        else:
            S.op('dve', lambda e: e.scalar_tensor_tensor(out=out, in0=in0, scalar=scalar, in1=in1, op0=op0, op1=op1,
                                                          accum_out=accum), r, w)

    def mm_group(bank, pairs, r=(), w=()):
        def fn(e):
            ins = None
            n = len(pairs)
            for i, (l_, r_) in enumerate(pairs):
                ins = e.matmul(psb[bank], lhsT=l_, rhs=r_, start=(i == 0), stop=(i == n - 1))
            return ins
        S.op('pe', fn, r, tuple(w) + (('ps', bank),))

    bank_rr = [0]

    def next_bank():
        b = bank_rr[0]
        bank_rr[0] = (b + 1) % 8
        return b

    VA = VEC.acc()

    ring_free = deque([0, 1, 2])
    ring_loaded = deque()
    next_load = [0]

    def prefetch(limit=99):
        while ring_free and next_load[0] < NG and limit > 0:
            limit -= 1
            s = ring_free.popleft()
            g = next_load[0]
            next_load[0] += 1
            src = wstream[g * P:(g + 1) * P, :].rearrange("p (a b) -> p a b", b=1024)
            dst = RING[s].ap.rearrange("p (a b) -> p a b", b=1024)
            S.dma('pool', lambda e, dst=dst, src=src: e.dma_start(out=dst, in_=src), f'ring{s}',
                  reads=(), writes=(RING[s].acc(),))
            ring_loaded.append((g, s))

    gptr = [0]

    def ring_get(expect_kind):
        prefetch()
        g, s = ring_loaded.popleft()
        assert g == gptr[0] and plan[g][0] == expect_kind, (g, gptr[0], plan[g], expect_kind)
        gptr[0] += 1
        return s

    def ring_release(s):
        ring_free.append(s)
        prefetch()

    out_toks = []
    bg = deque()
    lnq = deque()

    def drip(k=1):
        for _ in range(k):
            if lnq:
                lnq.popleft()[1]()

    xaq = deque()

    def drip_bg():
        if xaq:
            xaq.popleft()()
        elif bg:
            bg.popleft()()

    def flush_xa():
        while xaq:
            xaq.popleft()()

    def flush_ln(upto=NBLK - 1):
        while any(it[0] <= upto for it in lnq):
            lnq.popleft()[1]()

    def flush():
        while bg:
            bg.popleft()()

    def ln_enqueue(l, which, blk, last):
        th = ln_thunks(l, which, blk, last)
        zt, sa, sb, nt = th[0:C], th[C], th[C + 1], th[C + 2:]
        tail = []
        while lnq and lnq[-1][2] == 'norm' and lnq[-1][0] == blk - 1:
            tail.append(lnq.pop())
        tail.reverse()
        for t in zt:
            lnq.append((blk, t, 'z'))
        lnq.append((blk, sa, 'sa'))
        lnq.extend(tail)
        lnq.append((blk, sb, 'sb'))
        for t in nt:
            lnq.append((blk, t, 'norm'))

    S.dma('sp', lambda e: e.dma_start(out=VEC.ap, in_=vecs[:, :]), 'vecs', writes=(VA,))
    S.dma('pool', lambda e: e.dma_start(out=ONESM.ap, in_=onesm_d[:, :]), 'onesm', writes=(ONESM.acc(),))
    def load_x(blk):
        c0_, c1_ = blk * TB, (blk + 1) * TB
        for c in range(C):
            S.dma('sp', lambda e, c=c: e.dma_start(out=X[c].ap[:, c0_:c1_], in_=xT[c * P:(c + 1) * P, c0_:c1_]),
                  f'xl{c}_{blk}', writes=(X[c].acc(c0_, c1_),))
    load_x(0)
    prefetch(1)

    LNT = [Buf(arena, RU_LO + i * 2048, 2048, F32) for i in range(4)]
    ZB = [Buf(arena, RU_LO + 8192 + c * 1024, 1024, BF16) for c in range(C)]
    ZS = [Buf(arena, RU_LO + 16384 + c * 1024, 1024, BF16) for c in range(C)]
    SB = Buf(arena, RU_LO + 24576, 4096, F32)
    WM = Buf(arena, RU_LO + 28672, 4096, F32)
    RT = [Buf(arena, EXT_LO + i * 1024, 1024, BF16) for i in range(4)]
    GT = [Buf(arena, EXT_LO + 4096 + i * 2048, 2048, F32) for i in range(2)]

    WM2 = [WM, Buf(arena, EXT_LO + 4096, 4096, F32)]
    SBS = Buf(arena, EXT_LO, 4096, F32)
    WMB = [Buf(arena, R_LO + i * 16384, 16384, F32) for i in range(2)] + \
          [Buf(arena, RU_LO + i * 16384, 16384, F32) for i in range(2)]

    def mod_thunks(l, startup=False):
        th = []
        sb = SBS if startup else SB
        ma = MODL[l].acc()

        def t_sb():
            S.dma('sp', lambda e: e.dma_start(out=sb.ap, in_=cvec[:, :]), 'cv', writes=(sb.acc(),))
            act(sb.ap, sb.ap, AF.Silu, r=(sb.acc(),), w=(sb.acc(),))
        th.append(t_sb)
        if startup:
            for n4 in range(12):
                def t_n4(n4=n4):
                    wi = (n4 % 2) if n4 < 4 else 2 + (n4 % 2)
                    wb = WMB[wi]
                    row = (l * 48 + n4 * 4) * P
                    src = wmodT[row:row + 4 * P, :].rearrange("(i p) k -> p i k", p=P)
                    dst = wb.ap.rearrange("p (i k) -> p i k", k=D)
                    S.dma('sp', lambda e: e.dma_start(out=dst, in_=src), f'wmb{wi}', writes=(wb.acc(),))
                    for i in range(4):
                        n = n4 * 4 + i
                        w_ = wb.ap[:, i * D:(i + 1) * D]
                        stt(w_, w_, 1.0, sb.ap, ALU.mult, ALU.mult, r=(wb.acc(i * D, (i + 1) * D), sb.acc()),
                            w=(wb.acc(i * D, (i + 1) * D), ma), accum=MODL[l].ap[:, n:n + 1])
                th.append(t_n4)
        else:
            for n in range(48):
                def t_n(n=n):
                    wm = WM2[n % 2]
                    row = (l * 48 + n) * P
                    S.dma('sp', lambda e: e.dma_start(out=wm.ap, in_=wmodT[row:row + P, :]), f'wm{n % 2}',
                          writes=(wm.acc(),))
                    stt(wm.ap, wm.ap, 1.0, sb.ap, ALU.mult, ALU.mult, r=(wm.acc(), sb.acc()), w=(wm.acc(), ma),
                        accum=MODL[l].ap[:, n:n + 1])
                th.append(t_n)

        def t_fin(a=0, b=48):
            o = VL['bmod'][0] + l * 48
            tt(MODL[l].ap[:, a:b], MODL[l].ap[:, a:b], VEC.ap[:, o + a:o + b], ALU.add, r=(ma, VA), w=(ma,))
        if startup:
            fg = th[0:5] + [lambda: t_fin(0, 16)]
            bgt = th[5:] + [lambda: t_fin(16, 48)]
            return fg, bgt
        th.append(t_fin)
        return th

    def lnv(name, l, which):
        o = VL[name][0] + (l * 2 + which) * 8
        return VEC.ap[:, o:o + 8]

    def derived_A(l):
        rw = dict(r=LA(l) + (VA,), w=(DERL[l].acc(),))
        sh2, sc1, sc2, g1, g2 = (modcol(l, 3, 0, 8), modcol(l, 1, 0, 8), modcol(l, 4, 0, 8),
                                 modcol(l, 2, 0, 8), modcol(l, 5, 0, 8))
        ts(der(l, 'op1', 0, 8), sc1, 1.0, None, ALU.add, **rw)
        ts(der(l, 'op2', 0, 8), sc2, 1.0, None, ALU.add, **rw)
        ob2 = VL['b2'][0] + l * 8
        tt(der(l, 'g2b2', 0, 8), g2, VEC.ap[:, ob2:ob2 + 8], ALU.mult, **rw)
        G, B = lnv('lng', l, 0), lnv('lnb', l, 0)
        tt(der(l, 'ga1', 0, 8), G, der(l, 'op2', 0, 8), ALU.mult, **rw)
        tt(der(l, 'tmp', 0, 8), B, der(l, 'op2', 0, 8), ALU.mult, **rw)
        tt(der(l, 'gb1', 0, 8), der(l, 'tmp', 0, 8), sh2, ALU.add, **rw)
        ts(der(l, 'xg1', 0, 8), G, ALPHA, None, ALU.mult, **rw)
        stt(der(l, 'xb1', 0, 8), B, ALPHA, der(l, 'g2b2', 0, 8), ALU.mult, ALU.add, **rw)
        if l % 3 == 2:
            o = VL['pool_sc'][0]
            tt(der(l, 'z1s', 0, 8), g1, VEC.ap[:, o:o + 8], ALU.mult, **rw)
            ts(der(l, 'hps', 0, 8), der(l, 'op1', 0, 8), 1.0 / ALPHA, None, ALU.mult, **rw)
        else:
            ts(der(l, 'z1s', 0, 8), g1, 1.0, None, ALU.mult, **rw)

    def derived_B(l):
        rw = dict(r=LA(l) + (LA(l + 1) if l + 1 < n_layers else ()) + (VA,), w=(DERL[l].acc(),))
        G, B = lnv('lng', l, 1), lnv('lnb', l, 1)
        if l + 1 < n_layers:
            op1n, sh1n = der(l + 1, 'op1', 0, 8), modcol(l + 1, 0, 0, 8)
            tt(der(l, 'ga2', 0, 8), G, op1n, ALU.mult, **rw)
            tt(der(l, 'tmp', 0, 8), B, op1n, ALU.mult, **rw)
            tt(der(l, 'gb2', 0, 8), der(l, 'tmp', 0, 8), sh1n, ALU.add, **rw)
            ts(der(l, 'xg2', 0, 8), G, ALPHA, None, ALU.mult, **rw)
            ts(der(l, 'xb2', 0, 8), B, ALPHA, None, ALU.mult, **rw)
        else:
            ts(der(l, 'xg2', 0, 8), G, 1.0, None, ALU.mult, **rw)
            ts(der(l, 'xb2', 0, 8), B, 1.0, None, ALU.mult, **rw)

    m_ap = V('m')
    PT = Buf(arena, EXT_LO + 4096, 7 * 256, F32)
    PTAB = Buf(arena, EXT_LO + 6144, 2048, F32)
    act(FRQ.ap, V('jf', 0, 2), AF.Exp, scale=-math.log(10000.0) / 256.0, r=(VA,), w=(FRQ.acc(),))
    for c in range(C):
        part, jj = c // 2, c % 2
        n = 32 if part < 2 else 64
        base = V('posr', 0, 32) if part < 2 else V('posc', 0, 64)
        phase = 0.0 if part % 2 == 0 else math.pi / 2
        t1, qi_f, qf, t2, ng = (PT.ap[:, i * 64:i * 64 + n] for i in range(5))
        qi = PT.ap[:, 64:128].bitcast(I32)[:, 0:n]
        pa = (PT.acc(),)
        ts(t1, base, FRQ.ap[:, jj:jj + 1], phase + math.pi, ALU.mult, ALU.add, r=(VA, FRQ.acc()), w=pa)
        ts(qi, t1, 1.0 / (2 * math.pi), None, ALU.mult, r=pa, w=pa)
        S.op('dve', lambda e, qf=qf, qi=qi: e.tensor_copy(out=qf, in_=qi), pa, pa)
        stt(t2, qf, -2 * math.pi, t1, ALU.mult, ALU.add, r=pa, w=pa)
        ts(ng, t2, 0.0, 2 * math.pi, ALU.is_lt, ALU.mult, r=pa, w=pa)
        tt(t2, t2, ng, ALU.add, r=pa, w=pa)
        ts(t2, t2, 2 * math.pi - 1e-5, 0.0, ALU.min, ALU.max, r=pa, w=pa)
        act(t1, t2, AF.Sin, bias=-math.pi, r=pa, w=pa)
        ts(PTAB.ap[:, c * 64:c * 64 + n], t1, m_ap, None, ALU.mult, r=pa + (VA,), w=(PTAB.acc(),))

    def add_pos(blk):
        c0_, c1_ = blk * TB, (blk + 1) * TB
        for c in range(C):
            part = c // 2
            x3 = X[c].ap[:, c0_:c1_].rearrange("p (a b) -> p a b", b=64)
            if part < 2:
                bc = PTAB.ap[:, c * 64 + blk * 8:c * 64 + blk * 8 + 8].unsqueeze(2).broadcast_to([P, 8, 64])
            else:
                bc = PTAB.ap[:, c * 64:c * 64 + 64].unsqueeze(1).broadcast_to([P, 8, 64])
            tt(x3, x3, bc, ALU.add, r=(X[c].acc(c0_, c1_), PTAB.acc()), w=(X[c].acc(c0_, c1_),))

    add_pos(0)
    fg0, bg0 = mod_thunks(0, startup=True)
    for th in fg0:
        th()
    for blk in range(1, NBLK):
        load_x(blk)
    S.wait('pool', ('d:wmb0', 32))
    S.wait('pool', ('d:wmb1', 32))
    prefetch()
    ts(der(0, 'op1', 0, 8), modcol(0, 1, 0, 8), 1.0, None, ALU.add, r=LA(0), w=(DERL[0].acc(),))
    bg.extend(bg0)
    bg.append(lambda: derived_A(0))

    for blk in range(NBLK):
        c0_, c1_ = blk * TB, (blk + 1) * TB
        if blk > 0:
            add_pos(blk)
        for c in range(C):
            act(H[c].ap[:, c0_:c1_], X[c].ap[:, c0_:c1_], AF.Identity, bias=modcol(0, 0, c), scale=der(0, 'op1', c),
                r=(X[c].acc(c0_, c1_),) + LA(0), w=(H[c].acc(c0_, c1_),))
    for c in range(C):
        ts(X[c].ap, X[c].ap, ALPHA, None, ALU.mult, r=(X[c].acc(),), w=(X[c].acc(),))

    def ln_thunks(l, which, blk, last):
        th = []
        c0, c1 = blk * TB, (blk + 1) * TB
        MEAN, MSQ, RSTD, NMR = LNT
        ga, gb = ('ga1', 'gb1') if which == 0 else ('ga2', 'gb2')
        xg, xb = ('xg1', 'xb1') if which == 0 else ('xg2', 'xb2')
        bm, be = [None], [None]

        for c in range(C):
            def t_z(c=c):
                xa = X[c].ap[:, c0:c1]
                if ZB_ON_DVE:
                    S.op('dve', lambda e, c=c, xa=xa: e.tensor_copy(out=ZB[c].ap, in_=xa), (X[c].acc(c0, c1),), (ZB[c].acc(),))
                else:
                    act(ZB[c].ap, xa, AF.Copy, r=(X[c].acc(c0, c1),), w=(ZB[c].acc(),))
                act(ZS[c].ap, xa, AF.Square, r=(X[c].acc(c0, c1),), w=(ZS[c].acc(),))
            th.append(t_z)

        def t_stats():
            bm[0], be[0] = next_bank(), next_bank()
            mm_group(bm[0], [(ONESM.ap, ZB[c].ap) for c in range(C)], r=[ONESM.acc()] + [ZB[c].acc() for c in range(C)])
            mm_group(be[0], [(ONESM.ap, ZS[c].ap) for c in range(C)], r=[ONESM.acc()] + [ZS[c].acc() for c in range(C)])
            act(MEAN.ap, psb[bm[0]], AF.Copy, r=(('ps', bm[0]),), w=(MEAN.acc(),))
            tt(MSQ.ap, MEAN.ap, MEAN.ap, ALU.mult, r=(MEAN.acc(),), w=(MSQ.acc(),))
            tt(MSQ.ap, psb[be[0]], MSQ.ap, ALU.subtract, r=(('ps', be[0]), MSQ.acc()), w=(MSQ.acc(),))
            ts(MSQ.ap, MSQ.ap, 0.0, LN_EPS, ALU.max, ALU.add, r=(MSQ.acc(),), w=(MSQ.acc(),))
            act(MSQ.ap, MSQ.ap, AF.Ln, r=(MSQ.acc(),), w=(MSQ.acc(),))
        th.append(t_stats)

        def t_stats_b():
            act(RSTD.ap, MSQ.ap, AF.Exp, scale=-0.5, r=(MSQ.acc(),), w=(RSTD.acc(),))
            stt(NMR.ap, MEAN.ap, -1.0, RSTD.ap, ALU.mult, ALU.mult, r=(MEAN.acc(), RSTD.acc()), w=(NMR.acc(),))
        th.append(t_stats_b)

        for c in range(C):
            def t_n(c=c):
                xa = X[c].ap[:, c0:c1]
                xacc = X[c].acc(c0, c1)
                tt(xa, xa, RSTD.ap, ALU.mult, r=(xacc, RSTD.acc()), w=(xacc,))
                tt(xa, xa, NMR.ap, ALU.add, r=(xacc, NMR.acc()), w=(xacc,))
                if not last:
                    act(H[c].ap[:, c0:c1], xa, AF.Identity, bias=der(l, gb, c), scale=der(l, ga, c),
                        r=(xacc,) + LA(l), w=(H[c].acc(c0, c1),))
                if last and which == 1:
                    act(xa, xa, AF.Identity, bias=der(l, xb, c), scale=der(l, xg, c), r=(xacc,) + LA(l), w=(xacc,))
                elif blk == NBLK - 1:
                    xaq.append(lambda c=c: act(X[c].ap, X[c].ap, AF.Identity, bias=der(l, xb, c), scale=der(l, xg, c),
                                               r=(X[c].acc(),) + LA(l), w=(X[c].acc(),)))
                if which == 1 and (not last) and (l + 1) % 3 == 0 and FUSE_A and c < 6:
                    S.dma('sp', lambda e: e.dma_start(out=xsp[c * P:(c + 1) * P, c0:c1], in_=xa), f'xs{c}_{blk}',
                          reads=(xacc,))
                if last and which == 1:
                    out_toks.append(S.dma('sp', lambda e: e.dma_start(out=yT[c * P:(c + 1) * P, c0:c1], in_=xa),
                                          f'yo{c}', reads=(xacc,)))
            th.append(t_n)
        return th

    def z_evac(bank, l, m, blk, scale_ap):
        c0, c1 = blk * TB, (blk + 1) * TB
        xa = X[m].ap[:, c0:c1]
        stt(xa, psb[bank], scale_ap, xa, ALU.mult, ALU.add, r=(('ps', bank), X[m].acc(c0, c1)) + LA(l),
            w=(X[m].acc(c0, c1),))

    def out_proj(l):
        flush_xa()
        if OUTPROJ_BLK:
            ss = [ring_get('k8'), ring_get('k8')]
            for blk in range(NBLK):
                for m in range(C):
                    s_, mi = ss[m // 4], m % 4
                    b = next_bank()
                    mm_group(b, [(RING[s_].ap[:, kc * 512 + mi * P:kc * 512 + (mi + 1) * P],
                                  Y[kc].ap[:, blk * TB:(blk + 1) * TB]) for kc in range(C)],
                             r=[RING[s_].acc()] + [Y[kc].acc(blk * TB, (blk + 1) * TB) for kc in range(C)])
                    z_evac(b, l, m, blk, der(l, 'z1s', m))
                    drip()
                ln_enqueue(l, 0, blk, False)
            ring_release(ss[0])
            ring_release(ss[1])
            return
        s0 = ring_get('k8')
        for mi in range(4):
            for blk in range(NBLK):
                b = next_bank()
                mm_group(b, [(RING[s0].ap[:, kc * 512 + mi * P:kc * 512 + (mi + 1) * P],
                              Y[kc].ap[:, blk * TB:(blk + 1) * TB]) for kc in range(C)],
                         r=[RING[s0].acc()] + [Y[kc].acc(blk * TB, (blk + 1) * TB) for kc in range(C)])
                z_evac(b, l, mi, blk, der(l, 'z1s', mi))
                drip()
        ring_release(s0)
        s1 = ring_get('k8')
        for blk in range(NBLK):
            for mi in range(4):
                b = next_bank()
                mm_group(b, [(RING[s1].ap[:, kc * 512 + mi * P:kc * 512 + (mi + 1) * P],
                              Y[kc].ap[:, blk * TB:(blk + 1) * TB]) for kc in range(C)],
                         r=[RING[s1].acc()] + [Y[kc].acc(blk * TB, (blk + 1) * TB) for kc in range(C)])
                z_evac(b, l, 4 + mi, blk, der(l, 'z1s', 4 + mi))
                drip(LN_DRIP_OUT)
            ln_enqueue(l, 0, blk, False)
        ring_release(s1)

    def lru_mixer(l):
        j = l // 3
        RP = Buf(arena, EXT_LO, 8 * 259 * 4, F32)
        XCB = XCB1
        XC = Buf(arena, RU_LO, 8192, F32)
        A = Buf(arena, RU_LO + 8192, 8192, F32)
        UF = Buf(arena, RU_LO + 16384, 8192, F32)
        UB = Buf(arena, RU_LO + 24576, 8192, F32)
        SQ = XC
        sBD = ring_get('bd')
        BD = RING[sBD]
        lo = j * 64
        cn = LRD.ap[:, lo:lo + 16]
        cn2 = LRD.ap[:, lo + 16:lo + 32]
        la = (LRD.acc(),)
        ol = VL['lru_lam'][0] + j * 16
        act(cn, VEC.ap[:, ol:ol + 16], AF.Exp, scale=-1.0, r=(VA,), w=la)
        act(cn, cn, AF.Ln, bias=1.0, r=la, w=la)
        ts(cn2, cn, -2.0 * LRU_C, None, ALU.mult, r=la, w=la)
        ts(cn, cn, -LRU_C, None, ALU.mult, r=la, w=la)

        fused = FUSE_A and (l > 0 or FUSE_A_L0)
        if fused:
            flush_ln()
            flush()
        gi = [0]
        gts = GT if l == 0 else GT + [Buf(arena, EXT_LO + i * 2048, 2048, F32) for i in range(2)]
        for g2 in (() if fused else range(2)):
          s = ring_get('k8')
          for blk in range(NBLK):
            flush_ln(blk)
            c0, c1 = blk * TB, (blk + 1) * TB
            for cc in range(4):
                c = g2 * 4 + cc
                b = next_bank()
                mm_group(b, [(RING[s].ap[:, kc * 512 + cc * P:kc * 512 + (cc + 1) * P], H[kc].ap[:, c0:c1])
                             for kc in range(C)],
                         r=[RING[s].acc()] + [H[kc].acc(c0, c1) for kc in range(C)])
                g = gts[gi[0] % len(gts)]
                gi[0] += 1
                ga = (g.acc(),)
                pa = (('ps', b),)
                act(g.ap, psb[b], AF.Square, scale=math.sqrt(0.044715), r=pa, w=ga)
                stt(g.ap, g.ap, 1.0, psb[b], ALU.add, ALU.mult, r=ga + pa, w=ga)
                act(g.ap, g.ap, AF.Sigmoid, scale=GELU_K, r=ga, w=ga)
                tt(Y[c].ap[:, c0:c1], g.ap, psb[b], ALU.mult, r=ga + pa, w=(Y[c].acc(c0, c1),))
                drip_bg()
          ring_release(s)
        flush()
        flush_xa()
        NSP = 6
        for c in (range(NSP) if not (fused and l > 0) else ()):
            S.dma('sp', lambda e, c=c: e.dma_start(out=xsp[c * P:(c + 1) * P, :], in_=X[c].ap), f'x{c}',
                  reads=(X[c].acc(),))
        sets = [dict(RP=RP, XCB=XCB, XC=XC, A=A, UF=UF, UB=UB),
                dict(RP=Buf(arena, X_LO, 8 * 259 * 4, F32), XCB=Buf(arena, X_LO + 16896, 4096, BF16),
                     XC=Buf(arena, X_LO + 8704, 8192, F32), A=Buf(arena, X_LO + 8704 + 8192 + 4096, 8192, F32),
                     UF=Buf(arena, X_LO + 8704 + 16384 + 4096, 8192, F32),
                     UB=Buf(arena, X_LO + 8704 + 24576 + 4096, 8192, F32))]
        ocw = VL['lru_cw'][0] + j * 32
        ocb = VL['lru_cb'][0] + j * 8
        oba = VL['lru_ba'][0] + j * 16
        obx = VL['lru_bx'][0] + j * 16
        oh0 = VL['h0'][0] + j * 16

        def chunk_gen(c, st, s, cc, sg=None):
            RP_, XCB_, XC_, A_, UF_, UB_ = st['RP'], st['XCB'], st['XC'], st['A'], st['UF'], st['UB']
            SQ_ = XC_
            rp3 = RP_.ap.rearrange("p (s t) -> p s t", t=259)
            ra = (RP_.acc(),)
            if sg is not None:
                for blk in range(NBLK):
                    c0, c1 = blk * TB, (blk + 1) * TB
                    b = next_bank()
                    mm_group(b, [(RING[sg].ap[:, kc * 512 + cc * P:kc * 512 + (cc + 1) * P], H[kc].ap[:, c0:c1])
                                 for kc in range(C)],
                             r=[RING[sg].acc()] + [H[kc].acc(c0, c1) for kc in range(C)])
                    g_ap = UF_.ap[:, c0:c1]
                    ga = (UF_.acc(c0, c1),)
                    pa = (('ps', b),)
                    act(g_ap, psb[b], AF.Square, scale=math.sqrt(0.044715), r=pa, w=ga)
                    yield
                    stt(g_ap, g_ap, 1.0, psb[b], ALU.add, ALU.mult, r=ga + pa, w=ga)
                    yield
                    act(g_ap, g_ap, AF.Sigmoid, scale=GELU_K, r=ga, w=ga)
                    yield
                    tt(Y[c].ap[:, c0:c1], g_ap, psb[b], ALU.mult, r=ga + pa, w=(Y[c].acc(c0, c1),))
                    yield
            for blk in range(NBLK):
                c0, c1 = blk * TB, (blk + 1) * TB
                b = next_bank()
                mm_group(b, [(RING[s].ap[:, kc * 512 + cc * P:kc * 512 + (cc + 1) * P], H[kc].ap[:, c0:c1])
                             for kc in range(C)],
                         r=[RING[s].acc()] + [H[kc].acc(c0, c1) for kc in range(C)])
                act(rp3[:, 2 * blk:2 * blk + 2, 2:258], psb[b].rearrange("p (s t) -> p s t", t=256), AF.Copy,
                    r=(('ps', b),), w=ra)
                yield
            S.op('dve', lambda e: e.memset(rp3[:, 0:1, 0:2], 0.0), (), ra)
            S.op('dve', lambda e: e.memset(rp3[:, 7:8, 258:259], 0.0), (), ra)
            ts(rp3[:, 1:8, 0:2], rp3[:, 0:7, 256:258], m_ap, None, ALU.mult, r=ra + (VA,), w=ra)
            ts(rp3[:, 0:7, 258:259], rp3[:, 1:8, 2:3], m_ap, None, ALU.mult, r=ra + (VA,), w=ra)
            yield
            xc3 = XC_.ap.rearrange("p (s t) -> p s t", t=256)
            xa_ = (XC_.acc(),)
            cw = [VEC.ap[:, ocw + k * 8 + c:ocw + k * 8 + c + 1] for k in range(4)]
            act(xc3, rp3[:, :, 0:256], AF.Identity, bias=VEC.ap[:, ocb + c:ocb + c + 1], scale=cw[0], r=ra + (VA,), w=xa_)
            yield
            for k in range(1, 4):
                stt(xc3, rp3[:, :, k:k + 256], cw[k], xc3, ALU.mult, ALU.add, r=ra + xa_ + (VA,), w=xa_)
                yield
            act(XCB_.ap, XC_.ap, AF.Copy, r=xa_, w=(XCB_.acc(),))
            yield

            def gate(tp, dst, blk):
                c0, c1 = blk * TB, (blk + 1) * TB
                b = next_bank()
                mm_group(b, [(BD.ap[:, (tp * 8 + c) * P:(tp * 8 + c + 1) * P], XCB_.ap[:, c0:c1])],
                         r=[BD.acc(), XCB_.acc()])
                d = tp // 2
                ob = (oba if tp % 2 == 0 else obx) + d * 8 + c
                act(dst.ap[:, c0:c1], psb[b], AF.Sigmoid, bias=VEC.ap[:, ob:ob + 1], r=(('ps', b), VA),
                    w=(dst.acc(c0, c1),))

            AB_ = Buf(arena, RP_.lo, 8192, F32)
            for blk in range(NBLK):
                gate(1, UF_, blk)
                gate(3, UB_, blk)
                yield
                gate(0, A_, blk)
                gate(2, AB_, blk)
                yield
            tt(UF_.ap, UF_.ap, XC_.ap, ALU.mult, r=(UF_.acc(), XC_.acc()), w=(UF_.acc(),))
            yield
            tt(UB_.ap, UB_.ap, XC_.ap, ALU.mult, r=(UB_.acc(), XC_.acc()), w=(UB_.acc(),))
            yield 'mid'
            sa = (SQ_.acc(),)

            def cn_(d, two):
                o = lo + (16 if two else 0) + d * 8 + c
                return LRD.ap[:, o:o + 1]

            def h0_(d):
                return VEC.ap[:, oh0 + d * 8 + c:oh0 + d * 8 + c + 1]

            def so_(d):
                return ((j * 2 + d) * 8 + c) * 8
            fa_, ba_ = (A_.acc(),), (AB_.acc(),)
            ufa, uba = (UF_.acc(),), (UB_.acc(),)
            act(A_.ap, A_.ap, AF.Exp, scale=cn_(0, False), r=fa_ + la, w=fa_)
            yield
            act(AB_.ap, AB_.ap, AF.Exp, scale=cn_(1, False), r=ba_ + la, w=ba_)
            yield
            act(SQ_.ap, A_.ap, AF.Square, r=fa_, w=sa)
            yield
            ts(SQ_.ap, SQ_.ap, 1.0, None, ALU.min, r=sa, w=sa)
            yield
            act(SQ_.ap, SQ_.ap, AF.Sqrt, bias=1.0, scale=-1.0, r=sa, w=sa)
            yield
            tt(UF_.ap, UF_.ap, SQ_.ap, ALU.mult, r=ufa + sa, w=ufa)
            yield
            act(SQ_.ap, AB_.ap, AF.Square, r=ba_, w=sa)
            yield
            ts(A_.ap[:, 256:2048:256], A_.ap[:, 256:2048:256], m_ap, None, ALU.mult, r=fa_ + (VA,), w=fa_)
            S.op('dve', lambda e: e.tensor_tensor_scan(
                out=UF_.ap, data0=A_.ap, data1=UF_.ap, initial=h0_(0), op0=ALU.mult, op1=ALU.add),
                fa_ + ufa + (VA,), ufa)
            yield
            ts(SQ_.ap, SQ_.ap, 1.0, None, ALU.min, r=sa, w=sa)
            yield
            act(SQ_.ap, SQ_.ap, AF.Sqrt, bias=1.0, scale=-1.0, r=sa, w=sa)
            yield
            S.op('dve', lambda e: e.tensor_copy(out=STB.ap[:, so_(0):so_(0) + 8], in_=UF_.ap[:, 255:2048:256]),
                 ufa, (STB.acc(so_(0), so_(0) + 8),))
            tt(UB_.ap, UB_.ap, SQ_.ap, ALU.mult, r=uba + sa, w=uba)
            yield
            ts(AB_.ap[:, 255:1792:256], AB_.ap[:, 255:1792:256], m_ap, None, ALU.mult, r=ba_ + (VA,), w=ba_)
            S.op('dve', lambda e: e.tensor_tensor_scan(
                out=UB_.ap[:, ::-1], data0=AB_.ap[:, ::-1], data1=UB_.ap[:, ::-1], initial=h0_(1),
                op0=ALU.mult, op1=ALU.add), ba_ + uba + (VA,), uba)
            yield
            S.op('dve', lambda e: e.tensor_copy(out=STB.ap[:, so_(1):so_(1) + 8], in_=UB_.ap[:, 0:2048:256]),
                 uba, (STB.acc(so_(1), so_(1) + 8),))
            yield
            tt(UF_.ap, UF_.ap, UB_.ap, ALU.add, r=ufa + uba, w=ufa)
            yield
            tt(Y[c].ap, Y[c].ap, UF_.ap, ALU.mult, r=(Y[c].acc(), UF_.acc()), w=(Y[c].acc(),))
            yield

        if fused:
            gslots = [ring_get('k8'), None]
            slots = [ring_get('k8'), None]
        else:
            gslots = [None, None]
            slots = [ring_get('k8'), ring_get('k8')]
        pending = deque(range(C))
        active = []
        can_start = [True]
        while pending or active:
            if len(active) < 2 and pending and (can_start[0] or not active):
                c = pending.popleft()
                if fused and c == 4:
                    gslots[1] = ring_get('k8')
                    slots[1] = ring_get('k8')
                active.append((c, chunk_gen(c, sets[(c + 1) % 2], slots[c // 4], c % 4, gslots[c // 4])))
                can_start[0] = False
            for item in list(active):
                try:
                    if (next(item[1]) == 'mid' or not LRU_SKEW) and item is active[-1]:
                        can_start[0] = True
                        if fused and item[0] == 3:
                            ring_release(gslots[0])
                            ring_release(slots[0])
                except StopIteration:
                    active.remove(item)
                    if item[0] == 3 and not fused:
                        ring_release(slots[0])
                    if item[0] == 7:
                        ring_release(slots[1])
                        if fused:
                            ring_release(gslots[1])
                    if item[0] == 6:
                        for c2 in range(NSP):
                            S.dma('sp', lambda e, c2=c2: e.dma_start(out=X[c2].ap, in_=xsp[c2 * P:(c2 + 1) * P, :]),
                                  f'x{c2}', writes=(X[c2].acc(),))
        ring_release(sBD)
        out_proj(l)

    def sconv_mixer(l):
        BF_ = Buf(arena, RU_LO, 8192, F32)
        TT_ = Buf(arena, RU_LO + 8192, 8192, F32)
        VT = Buf(arena, RU_LO + 16384, 2048, F32)
        CV = Buf(arena, EXT_LO, 8 * 258 * 4, F32)
        cv3 = CV.ap.rearrange("p (s t) -> p s t", t=258)
        flush_ln()
        flush()
        S.op('dve', lambda e: e.memset(CV.ap, 0.0), (), (CV.acc(),))
        ocw = VL['sc_cw'][0]
        for c in range(C):
            s = ring_get('sc')
            for blk in range(NBLK):
                c0, c1 = blk * TB, (blk + 1) * TB
                bks = []
                for part in range(3):
                    b = next_bank()
                    bks.append(b)
                    mm_group(b, [(RING[s].ap[:, kc * 384 + part * P:kc * 384 + (part + 1) * P], H[kc].ap[:, c0:c1])
                                 for kc in range(C)],
                             r=[RING[s].acc()] + [H[kc].acc(c0, c1) for kc in range(C)])
                act(BF_.ap[:, c0:c1], psb[bks[0]], AF.Copy, r=(('ps', bks[0]),), w=(BF_.acc(c0, c1),))
                act(VT.ap, psb[bks[2]], AF.Copy, r=(('ps', bks[2]),), w=(VT.acc(),))
                tt(cv3[:, 2 * blk:2 * blk + 2, 1:257], psb[bks[1]].rearrange("p (s t) -> p s t", t=256),
                   VT.ap.rearrange("p (s t) -> p s t", t=256), ALU.mult, r=(('ps', bks[1]), VT.acc()), w=(CV.acc(),))
            ring_release(s)
            ca = (CV.acc(),)
            ts(cv3[:, 1:8, 0:1], cv3[:, 0:7, 256:257], m_ap, None, ALU.mult, r=ca + (VA,), w=ca)
            ts(cv3[:, 0:7, 257:258], cv3[:, 1:8, 1:2], m_ap, None, ALU.mult, r=ca + (VA,), w=ca)
            t3 = TT_.ap.rearrange("p (s t) -> p s t", t=256)
            ta = (TT_.acc(),)
            cw = [VEC.ap[:, ocw + k * 8 + c:ocw + k * 8 + c + 1] for k in range(3)]
            ts(t3, cv3[:, :, 0:256], cw[0], None, ALU.mult, r=ca + (VA,), w=ta)
            for k in (1, 2):
                stt(t3, cv3[:, :, k:k + 256], cw[k], t3, ALU.mult, ALU.add, r=ca + ta + (VA,), w=ta)
            tt(Y[c].ap, BF_.ap, TT_.ap, ALU.mult, r=(BF_.acc(), TT_.acc()), w=(Y[c].acc(),))
        out_proj(l)

    def pool_mixer(l):
        PW = 272
        NF = 8 * PW
        psets = [tuple(Buf(arena, base + i * 8704, 8 * PW * 4, F32) for i in range(3)) for base in (RU_LO, H_LO)]
        flush_ln()
        flush_xa()
        flush()
        for ps_ in psets:
            S.op('dve', lambda e, hp=ps_[0]: e.memset(hp.ap, 0.0), (), (ps_[0].acc(),))
        oF = VL['poolF'][0]
        fa = (FM.acc(),)
        ts(FM.ap[:, 64:65], m_ap, -1.0, 1.0, ALU.mult, ALU.add, r=(VA,), w=fa)
        ts(FM.ap[:, 0:64], VEC.ap[:, oF:oF + 64], FM.ap[:, 64:65], m_ap, ALU.mult, ALU.add, r=(VA,) + fa, w=fa)
        sW = ring_get('pool')

        def pool_gen(c, HP, PA, PB):
            hp3 = HP.ap.rearrange("p (s t) -> p s t", t=PW)
            ha, paa, pba = (HP.acc(),), (PA.acc(),), (PB.acc(),)
            g = c // 2
            w = POOL_WINDOWS[g]
            act(hp3[:, :, 8:264], X[c].ap.rearrange("p (s t) -> p s t", t=256), AF.Identity,
                bias=modcol(l, 0, c), scale=der(l, 'hps', c), r=(X[c].acc(),) + LA(l), w=ha)
            yield
            ts(hp3[:, 1:8, 0:8], hp3[:, 0:7, 256:264], m_ap, None, ALU.mult, r=ha + (VA,), w=ha)
            ts(hp3[:, 0:7, 264:272], hp3[:, 1:8, 8:16], m_ap, None, ALU.mult, r=ha + (VA,), w=ha)
            yield
            tt(PA.ap[:, 1:NF], HP.ap[:, 0:NF - 1], HP.ap[:, 1:NF], ALU.add, r=ha, w=paa)
            yield
            cur, cura, oth, otha = PA, paa, PB, pba
            vlo, vhi = 1, NF
            lvl = 2
            while lvl < w:
                hs = lvl // 2
                nlo, nhi = vlo + hs, vhi - hs
                tt(oth.ap[:, nlo:nhi], cur.ap[:, nlo - hs:nhi - hs], cur.ap[:, nlo + hs:nhi + hs], ALU.add,
                   r=cura, w=otha)
                yield
                cur, cura, oth, otha = oth, otha, cur, cura
                vlo, vhi = nlo, nhi
                lvl *= 2
            s3 = cur.ap.rearrange("p (s t) -> p s t", t=PW)
            hw = w // 2
            fo = oF + (g * 2) * 8
            FLg = VEC.ap[:, fo:fo + hw]
            FRg = VEC.ap[:, fo + 8:fo + 8 + hw - 1] if hw > 1 else None
            FmL = FM.ap[:, (g * 2) * 8:(g * 2) * 8 + hw]
            FmR = FM.ap[:, (g * 2 + 1) * 8:(g * 2 + 1) * 8 + hw - 1] if hw > 1 else None
            tt(s3[:, 0:1, 8:8 + hw], s3[:, 0:1, 8:8 + hw], FLg.unsqueeze(1), ALU.mult, r=cura + (VA,), w=cura)
            tt(s3[:, 1:8, 8:8 + hw], s3[:, 1:8, 8:8 + hw], FmL.unsqueeze(1).broadcast_to([P, 7, hw]), ALU.mult,
               r=cura + fa, w=cura)
            if hw > 1:
                e0 = 8 + 256 - (hw - 1)
                tt(s3[:, 7:8, e0:264], s3[:, 7:8, e0:264], FRg.unsqueeze(1), ALU.mult, r=cura + (VA,), w=cura)
                tt(s3[:, 0:7, e0:264], s3[:, 0:7, e0:264], FmR.unsqueeze(1).broadcast_to([P, 7, hw - 1]), ALU.mult,
                   r=cura + fa, w=cura)
            yield
            stt(Y[c].ap.rearrange("p (s t) -> p s t", t=256), s3[:, :, 8:264], 1.0 / w, hp3[:, :, 8:264],
                ALU.mult, ALU.subtract, r=cura + ha, w=(Y[c].acc(),))
            yield

        pend = deque(range(C))
        actv = []
        while pend or actv:
            while len(actv) < 2 and pend:
                c = pend.popleft()
                actv.append(pool_gen(c, *psets[c % 2]))
            for gnr in list(actv):
                try:
                    next(gnr)
                except StopIteration:
                    actv.remove(gnr)
        for blk in range(NBLK):
            c0, c1 = blk * TB, (blk + 1) * TB
            for m in range(C):
                g, mi = m // 2, m % 2
                b = next_bank()
                mm_group(b, [(RING[sW].ap[:, (g * 2 + kc) * 256 + mi * P:(g * 2 + kc) * 256 + (mi + 1) * P],
                              Y[g * 2 + kc].ap[:, c0:c1]) for kc in range(2)],
                         r=[RING[sW].acc()] + [Y[g * 2 + kc].acc(c0, c1) for kc in range(2)])
                z_evac(b, l, m, blk, der(l, 'z1s', m))
                drip()
            ln_enqueue(l, 0, blk, False)
        ring_release(sW)

    def mlp(l):
        last = (l == n_layers - 1)
        flush()
        if not last:
            bg.extend(mod_thunks(l + 1))
        ob1 = VL['b1'][0] + l * 32
        ri = [0]
        un = [0]

        def drip4():
            un[0] += 1
            if un[0] % 4 == 0:
                drip_bg()

        def w1_unit(s, fq, hf, jj, blk):
            jl = hf * 4 + jj
            j = fq * 8 + jl
            c0, c1 = blk * TB, (blk + 1) * TB
            b = next_bank()
            mm_group(b, [(RING[s].ap[:, kc * 512 + jj * P:kc * 512 + (jj + 1) * P], H[kc].ap[:, c0:c1])
                         for kc in range(C)],
                     r=[RING[s].acc()] + [H[kc].acc(c0, c1) for kc in range(C)])
            rt = RT[ri[0] % 4]
            ri[0] += 1
            act(rt.ap, psb[b], AF.Relu, bias=VEC.ap[:, ob1 + j:ob1 + j + 1], r=(('ps', b), VA), w=(rt.acc(),))
            tt(HID[jl].ap[:, c0:c1], rt.ap, rt.ap, ALU.mult, r=(rt.acc(),), w=(HID[jl].acc(c0, c1),))
            drip4()

        for fq in range(4):
            if fq == 0 and not FQ0_BLK:
                flush_ln()
            if fq == 0 and FQ0_BLK:
                ss = [ring_get('k8'), ring_get('k8')]
                for blk in range(NBLK):
                    flush_ln(blk)
                    for hf in range(2):
                        for jj in range(4):
                            w1_unit(ss[hf], fq, hf, jj, blk)
                ring_release(ss[0])
                ring_release(ss[1])
            for hf in (range(2) if (fq > 0 or not FQ0_BLK) else ()):
                s = ring_get('k8')
                for jj in range(4):
                    jl = hf * 4 + jj
                    j = fq * 8 + jl
                    for blk in range(NBLK):
                        c0, c1 = blk * TB, (blk + 1) * TB
                        b = next_bank()
                        mm_group(b, [(RING[s].ap[:, kc * 512 + jj * P:kc * 512 + (jj + 1) * P], H[kc].ap[:, c0:c1])
                                     for kc in range(C)],
                                 r=[RING[s].acc()] + [H[kc].acc(c0, c1) for kc in range(C)])
                        rt = RT[ri[0] % 4]
                        ri[0] += 1
                        act(rt.ap, psb[b], AF.Relu, bias=VEC.ap[:, ob1 + j:ob1 + j + 1], r=(('ps', b), VA), w=(rt.acc(),))
                        tt(HID[jl].ap[:, c0:c1], rt.ap, rt.ap, ALU.mult, r=(rt.acc(),), w=(HID[jl].acc(c0, c1),))
                        drip4()
                ring_release(s)
            flush_xa()
            if fq == 3:
                flush()
                if not last:
                    derived_A(l + 1)
                derived_B(l)
                s0 = ring_get('k8')
                s1 = ring_get('k8')
                for blk in range(NBLK):
                    c0, c1 = blk * TB, (blk + 1) * TB
                    for m in range(C):
                        s, mi = (s0, m) if m < 4 else (s1, m - 4)
                        b = next_bank()
                        mm_group(b, [(RING[s].ap[:, kc * 512 + mi * P:kc * 512 + (mi + 1) * P], HID[kc].ap[:, c0:c1])
                                     for kc in range(C)],
                                 r=[RING[s].acc()] + [HID[kc].acc(c0, c1) for kc in range(C)])
                        z_evac(b, l, m, blk, modcol(l, 5, m))
                        drip(LN_DRIP_W2)
                    ln_enqueue(l, 1, blk, last)
                ring_release(s0)
                ring_release(s1)
            else:
                for mq in range(2):
                    s = ring_get('k8')
                    for mi in range(4):
                        m = mq * 4 + mi
                        for blk in range(NBLK):
                            c0, c1 = blk * TB, (blk + 1) * TB
                            b = next_bank()
                            mm_group(b, [(RING[s].ap[:, kc * 512 + mi * P:kc * 512 + (mi + 1) * P], HID[kc].ap[:, c0:c1])
                                         for kc in range(C)],
                                     r=[RING[s].acc()] + [HID[kc].acc(c0, c1) for kc in range(C)])
                            z_evac(b, l, m, blk, modcol(l, 5, m))
                            drip4()
                    ring_release(s)
        flush()
        if last:
            flush_ln()

    for l in range(n_layers):
        kind = l % 3
        if kind == 0:
            lru_mixer(l)
        elif kind == 1:
            sconv_mixer(l)
        else:
            pool_mixer(l)
        mlp(l)

    flush_xa()
    toks = {}
    for tk in out_toks:
        toks[tk[0]] = max(toks.get(tk[0], 0), tk[1])
    tk = S.dma('sp', lambda e: e.dma_start(out=st_d[:, :], in_=STB.ap), 'vecs', reads=(STB.acc(),))
    toks[tk[0]] = tk[1]
    for k, v in toks.items():
        S.wait('sp', (k, v))

    semnames = list(ENGS) + sorted({it[2] for e in ENGS for it in S.prog[e] if it[0] == 'd'})
    sems = {n: es.enter_context(nc.semaphore(n.replace(':', '_'))) for n in semnames}

    def run(name, e):
        for it in S.prog[name]:
            if it[0] == 'w':
                e.wait_ge(sems[it[1]], it[2])
            elif it[0] == 'o':
                it[1](e).then_inc(sems[name], 1)
            else:
                it[1](e).then_inc(sems[it[2]], 16)

    with nc.Block() as block:
        @block.tensor
        def _(e):
            run('pe', e)

        @block.scalar
        def _(e):
            run('act', e)

        @block.vector
        def _(e):
            run('dve', e)

        @block.gpsimd
        def _(e):
            run('pool', e)

        @block.sync
        def _(e):
            run('sp', e)
    es.close()
    return nc, plan


def make_inputs(inp, n_layers=DEPTH):
    plan = weight_plan(n_layers)
    wstream = np.concatenate([pack_group(g, inp) for g in plan], axis=0)
    wmodT = np.ascontiguousarray(np.transpose(inp['w_mod'], (0, 2, 1))).reshape(4 * 6144, D)
    onesm = np.full((P, P), 1.0 / D, np.float32)

    base = np.zeros((P, NV), np.float32)

    def put(name, arr):
        o, n = VL[name]
        base[:, o:o + n] = np.asarray(arr, np.float32).reshape(P, n)

    put('bmod', fm(inp['b_mod']))
    put('lng', fm(inp['ln_g']))
    put('lnb', fm(inp['ln_b']))
    put('b1', fm(inp['mlp_b1']))
    put('b2', fm(inp['mlp_b2']))
    put('lru_cw', fm(inp['lru_conv_w']))
    put('lru_cb', fm(inp['lru_conv_b']))
    put('lru_ba', fm(inp['lru_b_a']))
    put('lru_bx', fm(inp['lru_b_x']))
    put('lru_lam', fm(inp['lru_lambda']))
    put('sc_cw', fm(inp['sc_conv_w'][0]))
    put('pool_sc', fm(inp['pool_scale'][0]))
    jf = np.stack([np.arange(P), np.arange(P) + P], axis=1).astype(np.float32)
    put('jf', jf)
    put('posr', np.broadcast_to(np.arange(32, dtype=np.float32), (P, 32)))
    put('posc', np.broadcast_to(np.arange(64, dtype=np.float32), (P, 64)))
    F = np.ones((4, 2, 8), np.float32)
    for g, w in enumerate(POOL_WINDOWS):
        hw = w // 2
        for t in range(hw):
            F[g, 0, t] = w / float(t + hw)
        for i in range(hw - 1):
            t = 256 - (hw - 1) + i
            F[g, 1, i] = w / float(256 - t + hw)
    put('poolF', np.broadcast_to(F.reshape(1, 64), (P, 64)))

    in_maps = []
    for core in range(NCORES):
        v = base.copy()
        xT = np.zeros((D, T), np.float32)
        if core in SAMPLE_CORES:
            b = SAMPLE_CORES[core]
            xT[:, :] = inp['x_sample'][b].T
            o, n = VL['h0']
            v[:, o:o + n] = fm(inp['state_lru'][b]).reshape(P, n)
            v[:, VL['m'][0]] = 1.0
            cond = inp['c'][b]
        elif core in PROMPT_CORES:
            for si, sq in enumerate(PROMPT_CORES[core]):
                xT[:, si * L:(si + 1) * L] = inp['x_prompt'][sq].T
            cond = inp['c_ctx']
        else:
            v[:] = 0.0
            cond = np.zeros((D,), np.float32)
        in_maps.append({
            "xT": xT,
            "cvec": np.ascontiguousarray(np.broadcast_to(np.asarray(cond, np.float32), (P, D))),
            "vecs": v,
            "wmodT": wmodT,
            "wstream": wstream,
            "onesm": onesm,
        })
    return in_maps


_CACHE = {}


def run_device(inp, n_layers=DEPTH, trace=False):
    if n_layers not in _CACHE:
        _CACHE[n_layers] = build(n_layers)
    nc, _ = _CACHE[n_layers]
    in_maps = make_inputs(inp, n_layers)
    kw = dict(trace=True) if trace else {}
    return run_bass_kernel_spmd(nc, in_maps, core_ids=list(range(NCORES)), **kw)


def assemble(res):
    B, SEQ = 16, 256
    y_prompt = np.zeros((B, SEQ, D), np.float32)
    y_sample = np.zeros((2, T, D), np.float32)
    new_state = np.zeros((B, 2, 2, D), np.float32)
    for core in range(NCORES):
        r = res.results[core]
        y = np.asarray(r["yT"]).T
        if core in SAMPLE_CORES:
            y_sample[SAMPLE_CORES[core]] = y
        elif core in PROMPT_CORES:
            st = np.asarray(r["st"]).reshape(P, 2, 2, 8, 8)
            for si, sq in enumerate(PROMPT_CORES[core]):
                y_prompt[sq] = y[si * L:(si + 1) * L]
                new_state[sq] = np.transpose(st[:, :, :, :, si], (1, 2, 3, 0)).reshape(2, 2, D)
    return y_prompt, y_sample, new_state


def kernel(**inputs):
    inp = {k: np.asarray(v) for k, v in inputs.items()}
    res = run_device(inp)
    return assemble(res)
```

```python
import math
from collections import deque
from contextlib import ExitStack

import numpy as np
import concourse.bass as bass
import concourse.mybir as mybir
from concourse.bass_utils import run_bass_kernel_spmd

F32 = mybir.dt.float32
BF16 = mybir.dt.bfloat16
I32 = mybir.dt.int32
ALU = mybir.AluOpType
AF = mybir.ActivationFunctionType

P = 128
D = 1024
C = 8
T = 2048
NBLK = 4
TB = 512
NS = 8
L = 256
DFF = 4096
DEPTH = 4
ALPHA = (2.0 * DEPTH) ** 0.25
LN_EPS = 1e-5
LRU_C = 8.0
NCORES = 8
POOL_WINDOWS = (2, 4, 8, 16)
GELU_K = 2.0 * math.sqrt(2.0 / math.pi)
LRU_SKEW = True
ZB_ON_DVE = True
FQ0_BLK = False
OUTPROJ_BLK = False
LN_DRIP_OUT = 5
LN_DRIP_W2 = 3
FUSE_A = False
FUSE_A_L0 = False

SAMPLE_CORES = {0: 0, 1: 1}
PROMPT_CORES = {2: [0, 1, 2], 3: [3, 4, 5], 4: [6, 7, 8], 5: [9, 10, 11], 6: [12, 13], 7: [14, 15]}

VL = {}
_cur = 0


def _add(name, n):
    global _cur
    VL[name] = (_cur, n)
    _cur += n


_add('bmod', 4 * 48)
_add('lng', 4 * 2 * 8)
_add('lnb', 4 * 2 * 8)
_add('b1', 4 * 32)
_add('b2', 4 * 8)
_add('lru_cw', 2 * 4 * 8)
_add('lru_cb', 2 * 8)
_add('lru_ba', 2 * 2 * 8)
_add('lru_bx', 2 * 2 * 8)
_add('lru_lam', 2 * 2 * 8)
_add('sc_cw', 3 * 8)
_add('pool_sc', 8)
_add('h0', 2 * 2 * 8)
_add('m', 1)
_add('jf', 2)
_add('posr', 32)
_add('posc', 64)
_add('poolF', 4 * 2 * 8)
NV = _cur


def fm(v):
    v = np.asarray(v, np.float32)
    lead = v.shape[:-1]
    n = v.shape[-1] // P
    return np.moveaxis(v.reshape(*lead, n, P), -1, 0)


def weight_plan(n_layers=DEPTH):
    plan = []
    for l in range(n_layers):
        kind, j = l % 3, l // 3
        if kind == 0:
            plan.append(('bd', j))
            fused = FUSE_A and (l > 0 or FUSE_A_L0)
            for c0 in ((0, 1024, 512, 1536) if fused else (0, 512, 1024, 1536)):
                plan.append(('k8', 'lru_w_in', j, 0, c0))
            for c0 in (0, 512):
                plan.append(('k8', 'lru_w_out', j, 0, c0))
        elif kind == 1:
            for c in range(C):
                plan.append(('sc', c))
            for c0 in (0, 512):
                plan.append(('k8', 'sc_w_out', j, 0, c0))
        else:
            plan.append(('pool', j))
        for fq in range(4):
            for hf in range(2):
                plan.append(('k8', 'mlp_w1', l, 0, fq * 1024 + hf * 512))
            for mq in range(2):
                plan.append(('k8', 'mlp_w2', l, fq * 1024, mq * 512))
    return plan


def pack_group(g, inp):
    out = np.zeros((P, 4096), np.float32)
    if g[0] == 'k8':
        _, name, idx, r0, c0 = g
        W = inp[name][idx][r0:r0 + 1024, c0:c0 + 512]
        out[:, :] = W.reshape(8, P, 512).transpose(1, 0, 2).reshape(P, 4096)
    elif g[0] == 'sc':
        c = g[1]
        W = inp['sc_w_in'][0]
        W3 = np.concatenate([W[:, k * 1024 + c * P:k * 1024 + (c + 1) * P] for k in range(3)], axis=1)
        out[:, :3072] = W3.reshape(8, P, 384).transpose(1, 0, 2).reshape(P, 3072)
    elif g[0] == 'bd':
        j = g[1]
        t = 0
        for d in range(2):
            for nm in ('lru_w_a', 'lru_w_x'):
                w = inp[nm][j, d]
                for c in range(C):
                    blk = np.zeros((P, P), np.float32)
                    blk[0:64, 0:64] = w[2 * c]
                    blk[64:128, 64:128] = w[2 * c + 1]
                    out[:, (t * 8 + c) * P:(t * 8 + c + 1) * P] = blk
                t += 1
    elif g[0] == 'pool':
        w = inp['pool_w'][g[1]]
        for gg in range(4):
            for kc in range(2):
                out[:, (gg * 2 + kc) * 256:(gg * 2 + kc + 1) * 256] = w[gg, kc * P:(kc + 1) * P, :]
    return out


ENGS = ['pe', 'act', 'dve', 'pool', 'sp']
GR = 512


class Sched:
    def __init__(self):
        self.prog = {e: [] for e in ENGS}
        self.cnt = {e: 0 for e in ENGS}
        self.known = {e: {} for e in ENGS}
        self.gran = {}
        self.dmacnt = {}
        self.snap = {}
        self.seq = {}
        self.nseq = 0

    def _keys(self, acc):
        if acc[0] == 'ps':
            return [('ps', acc[1])]
        _, lo, hi = acc
        return [('sb', k) for k in range(lo // GR, (hi - 1) // GR + 1)]

    def _collect(self, reads, writes):
        need = {}

        def upd(tok):
            if tok is not None and need.get(tok[0], 0) < tok[1]:
                need[tok[0]] = tok[1]

        for a in reads:
            for k in self._keys(a):
                g = self.gran.get(k)
                if g:
                    upd(g[0])
        for a in writes:
            for k in self._keys(a):
                g = self.gran.get(k)
                if g:
                    upd(g[0])
                    for s, v in g[1].items():
                        upd((s, v))
        return need

    def _waits(self, eng, need):
        kn = self.known[eng]
        for s, v in sorted(need.items(), key=lambda sv: -self.seq.get(sv, 0)):
            if kn.get(s, 0) < v:
                self.prog[eng].append(('w', s, v))
                kn[s] = v
                sn = self.snap.get((s, v))
                if sn:
                    for s2, v2 in sn.items():
                        if kn.get(s2, 0) < v2:
                            kn[s2] = v2

    def _record(self, tok, reads, writes):
        for a in reads:
            for k in self._keys(a):
                g = self.gran.setdefault(k, [None, {}])
                if g[1].get(tok[0], 0) < tok[1]:
                    g[1][tok[0]] = tok[1]
        for a in writes:
            for k in self._keys(a):
                self.gran[k] = [tok, {}]

    def op(self, eng, fn, reads=(), writes=()):
        self._waits(eng, self._collect(reads, writes))
        self.cnt[eng] += 1
        tok = (eng, self.cnt[eng])
        self.snap[tok] = dict(self.known[eng])
        self.nseq += 1
        self.seq[tok] = self.nseq
        self.prog[eng].append(('o', fn))
        self._record(tok, reads, writes)

    def dma(self, q, fn, sem, reads=(), writes=()):
        self._waits(q, self._collect(reads, writes))
        self.dmacnt[sem] = self.dmacnt.get(sem, 0) + 1
        tok = ('d:' + sem, 16 * self.dmacnt[sem])
        self.snap[tok] = dict(self.known[q])
        self.nseq += 1
        self.seq[tok] = self.nseq
        self.prog[q].append(('d', fn, 'd:' + sem))
        self._record(tok, reads, writes)
        return tok

    def wait(self, eng, tok):
        self._waits(eng, {tok[0]: tok[1]})


class Buf:
    def __init__(self, arena, lo, nbytes, dt):
        assert lo % 4 == 0 and nbytes % 4 == 0
        self.lo, self.nb, self.dt = lo, nbytes, dt
        self.es = 4 if dt in (F32, I32) else 2
        a = arena[:, lo // 4:(lo + nbytes) // 4]
        self.ap = a if dt == F32 else a.bitcast(dt)

    def acc(self, a=None, b=None):
        if a is None:
            return ('sb', self.lo, self.lo + self.nb)
        return ('sb', self.lo + a * self.es, self.lo + b * self.es)


def build(n_layers=DEPTH):
    nc = bass.Bass("TRN2", target_bir_lowering=False)
    plan = weight_plan(n_layers)
    NG = len(plan)

    xT = nc.dram_tensor("xT", [D, T], F32, kind="ExternalInput").ap()
    cvec = nc.dram_tensor("cvec", [P, D], F32, kind="ExternalInput").ap()
    vecs = nc.dram_tensor("vecs", [P, NV], F32, kind="ExternalInput").ap()
    wmodT = nc.dram_tensor("wmodT", [4 * 6144, D], F32, kind="ExternalInput").ap()
    wstream = nc.dram_tensor("wstream", [NG * P, 4096], F32, kind="ExternalInput").ap()
    onesm_d = nc.dram_tensor("onesm", [P, P], F32, kind="ExternalInput").ap()
    yT = nc.dram_tensor("yT", [D, T], F32, kind="ExternalOutput").ap()
    st_d = nc.dram_tensor("st", [P, 256], F32, kind="ExternalOutput").ap()
    xsp = nc.dram_tensor("xsp", [D, T], F32).ap()

    S = Sched()
    es = ExitStack()
    ARENA_F = 53184
    arena_t = es.enter_context(nc.sbuf_tensor("arena", [P, ARENA_F], F32))
    arena = arena_t[:]
    psb = [es.enter_context(nc.psum_tensor(f"ps{k}", [P, 512], F32))[:] for k in range(8)]

    X_LO, H_LO, R_LO, RING_LO, EXT_LO, SM_LO = 0, 65536, 98304, 163840, 188416, 196736
    RU_LO = R_LO + 32768
    X = [Buf(arena, X_LO + c * 8192, 8192, F32) for c in range(C)]
    H = [Buf(arena, H_LO + c * 4096, 4096, BF16) for c in range(C)]
    Y = [Buf(arena, R_LO + c * 4096, 4096, BF16) for c in range(C)]
    HID = Y
    RING = [Buf(arena, RING_LO + s * 8192, 8192, BF16) for s in range(3)]
    EXT = Buf(arena, EXT_LO, 8320, F32)
    sm_cur = [SM_LO]

    def small(nfloats, dt=F32, align=4):
        sm_cur[0] = (sm_cur[0] + align - 1) // align * align
        b = Buf(arena, sm_cur[0], nfloats * 4, dt)
        sm_cur[0] += nfloats * 4
        assert sm_cur[0] <= ARENA_F * 4
        return b

    MODL = [small(128, align=512) for _ in range(4)]
    DERL = [small(128, align=512) for _ in range(4)]
    VEC = small(NV, align=512)
    LRD = small(2 * 64)
    FRQ = small(2)
    FM = small(72)
    STB = small(256)
    ONESM = small(64, BF16)
    XCB1 = small(1024, BF16, align=512)

    def V(name, off=0, n=1):
        o = VL[name][0] + off
        return VEC.ap[:, o:o + n]

    def modcol(l, kind, c=0, n=1):
        o = kind * 8 + c
        return MODL[l].ap[:, o:o + n]

    def LA(l):
        return (MODL[l].acc(), DERL[l].acc())

    DN = {'op1': 0, 'op2': 1, 'g2b2': 2, 'ga1': 3, 'gb1': 4, 'xg1': 5, 'xb1': 6, 'z1s': 7,
          'ga2': 8, 'gb2': 9, 'xg2': 10, 'xb2': 11, 'hps': 12, 'tmp': 13}

    def der(l, name, c=0, n=1):
        o = DN[name] * 8 + c
        return DERL[l].ap[:, o:o + n]

    def act(out, in_, func, bias=0.0, scale=1.0, r=(), w=()):
        S.op('act', lambda e: e.activation(out=out, in_=in_, func=func, bias=bias, scale=scale), r, w)

    def tt(out, in0, in1, op, r=(), w=(), eng='dve'):
        S.op(eng, lambda e: e.tensor_tensor(out=out, in0=in0, in1=in1, op=op), r, w)

    def ts(out, in0, s1, s2, op0, op1=None, r=(), w=(), eng='dve'):
        if op1 is None:
            S.op(eng, lambda e: e.tensor_scalar(out=out, in0=in0, scalar1=s1, scalar2=None, op0=op0), r, w)
        else:
            S.op(eng, lambda e: e.tensor_scalar(out=out, in0=in0, scalar1=s1, scalar2=s2, op0=op0, op1=op1), r, w)

    def stt(out, in0, scalar, in1, op0, op1, r=(), w=(), accum=None):
        if accum is None:
            S.op('dve', lambda e: e.scalar_tensor_tensor(out=out, in0=in0, scalar=scalar, in1=in1, op0=op0, op1=op1), r, w)
        else:
            S.op('dve', lambda e: e.scalar_tensor_tensor(out=out, in0=in0, scalar=scalar, in1=in1, op0=op0, op1=op1,
                                                          accum_out=accum), r, w)

    def mm_group(bank, pairs, r=(), w=()):
        def fn(e):
            ins = None
            n = len(pairs)
            for i, (l_, r_) in enumerate(pairs):
                ins = e.matmul(psb[bank], lhsT=l_, rhs=r_, start=(i == 0), stop=(i == n - 1))
            return ins
        S.op('pe', fn, r, tuple(w) + (('ps', bank),))

    bank_rr = [0]

    def next_bank():
        b = bank_rr[0]
        bank_rr[0] = (b + 1) % 8
        return b

    VA = VEC.acc()

    ring_free = deque([0, 1, 2])
    ring_loaded = deque()
    next_load = [0]

    def prefetch(limit=99):
        while ring_free and next_load[0] < NG and limit > 0:
            limit -= 1
            s = ring_free.popleft()
            g = next_load[0]
            next_load[0] += 1
            src = wstream[g * P:(g + 1) * P, :].rearrange("p (a b) -> p a b", b=1024)
            dst = RING[s].ap.rearrange("p (a b) -> p a b", b=1024)
            S.dma('pool', lambda e, dst=dst, src=src: e.dma_start(out=dst, in_=src), f'ring{s}',
                  reads=(), writes=(RING[s].acc(),))
            ring_loaded.append((g, s))

    gptr = [0]

    def ring_get(expect_kind):
        prefetch()
        g, s = ring_loaded.popleft()
        assert g == gptr[0] and plan[g][0] == expect_kind, (g, gptr[0], plan[g], expect_kind)
        gptr[0] += 1
        return s

    def ring_release(s):
        ring_free.append(s)
        prefetch()

    out_toks = []
    bg = deque()
    lnq = deque()

    def drip(k=1):
        for _ in range(k):
            if lnq:
                lnq.popleft()[1]()

    xaq = deque()

    def drip_bg():
        if xaq:
            xaq.popleft()()
        elif bg:
            bg.popleft()()

    def flush_xa():
        while xaq:
            xaq.popleft()()

    def flush_ln(upto=NBLK - 1):
        while any(it[0] <= upto for it in lnq):
            lnq.popleft()[1]()

    def flush():
        while bg:
            bg.popleft()()

    def ln_enqueue(l, which, blk, last):
        th = ln_thunks(l, which, blk, last)
        zt, sa, sb, nt = th[0:C], th[C], th[C + 1], th[C + 2:]
        tail = []
        while lnq and lnq[-1][2] == 'norm' and lnq[-1][0] == blk - 1:
            tail.append(lnq.pop())
        tail.reverse()
        for t in zt:
            lnq.append((blk, t, 'z'))
        lnq.append((blk, sa, 'sa'))
        lnq.extend(tail)
        lnq.append((blk, sb, 'sb'))
        for t in nt:
            lnq.append((blk, t, 'norm'))

    S.dma('sp', lambda e: e.dma_start(out=VEC.ap, in_=vecs[:, :]), 'vecs', writes=(VA,))
    S.dma('pool', lambda e: e.dma_start(out=ONESM.ap, in_=onesm_d[:, :]), 'onesm', writes=(ONESM.acc(),))
    def load_x(blk):
        c0_, c1_ = blk * TB, (blk + 1) * TB
        for c in range(C):
            S.dma('sp', lambda e, c=c: e.dma_start(out=X[c].ap[:, c0_:c1_], in_=xT[c * P:(c + 1) * P, c0_:c1_]),
                  f'xl{c}_{blk}', writes=(X[c].acc(c0_, c1_),))
    load_x(0)
    prefetch(1)

    LNT = [Buf(arena, RU_LO + i * 2048, 2048, F32) for i in range(4)]
    ZB = [Buf(arena, RU_LO + 8192 + c * 1024, 1024, BF16) for c in range(C)]
    ZS = [Buf(arena, RU_LO + 16384 + c * 1024, 1024, BF16) for c in range(C)]
    SB = Buf(arena, RU_LO + 24576, 4096, F32)
    WM = Buf(arena, RU_LO + 28672, 4096, F32)
    RT = [Buf(arena, EXT_LO + i * 1024, 1024, BF16) for i in range(4)]
    GT = [Buf(arena, EXT_LO + 4096 + i * 2048, 2048, F32) for i in range(2)]

    WM2 = [WM, Buf(arena, EXT_LO + 4096, 4096, F32)]
    SBS = Buf(arena, EXT_LO, 4096, F32)
    WMB = [Buf(arena, R_LO + i * 16384, 16384, F32) for i in range(2)] + \
          [Buf(arena, RU_LO + i * 16384, 16384, F32) for i in range(2)]

    def mod_thunks(l, startup=False):
        th = []
        sb = SBS if startup else SB
        ma = MODL[l].acc()

        def t_sb():
            S.dma('sp', lambda e: e.dma_start(out=sb.ap, in_=cvec[:, :]), 'cv', writes=(sb.acc(),))
            act(sb.ap, sb.ap, AF.Silu, r=(sb.acc(),), w=(sb.acc(),))
        th.append(t_sb)
        if startup:
            for n4 in range(12):
                def t_n4(n4=n4):
                    wi = (n4 % 2) if n4 < 4 else 2 + (n4 % 2)
                    wb = WMB[wi]
                    row = (l * 48 + n4 * 4) * P
                    src = wmodT[row:row + 4 * P, :].rearrange("(i p) k -> p i k", p=P)
                    dst = wb.ap.rearrange("p (i k) -> p i k", k=D)
                    S.dma('sp', lambda e: e.dma_start(out=dst, in_=src), f'wmb{wi}', writes=(wb.acc(),))
                    for i in range(4):
                        n = n4 * 4 + i
                        w_ = wb.ap[:, i * D:(i + 1) * D]
                        stt(w_, w_, 1.0, sb.ap, ALU.mult, ALU.mult, r=(wb.acc(i * D, (i + 1) * D), sb.acc()),
                            w=(wb.acc(i * D, (i + 1) * D), ma), accum=MODL[l].ap[:, n:n + 1])
                th.append(t_n4)
        else:
            for n in range(48):
                def t_n(n=n):
                    wm = WM2[n % 2]
                    row = (l * 48 + n) * P
                    S.dma('sp', lambda e: e.dma_start(out=wm.ap, in_=wmodT[row:row + P, :]), f'wm{n % 2}',
                          writes=(wm.acc(),))
                    stt(wm.ap, wm.ap, 1.0, sb.ap, ALU.mult, ALU.mult, r=(wm.acc(), sb.acc()), w=(wm.acc(), ma),
                        accum=MODL[l].ap[:, n:n + 1])
                th.append(t_n)

        def t_fin(a=0, b=48):
            o = VL['bmod'][0] + l * 48
            tt(MODL[l].ap[:, a:b], MODL[l].ap[:, a:b], VEC.ap[:, o + a:o + b], ALU.add, r=(ma, VA), w=(ma,))
        if startup:
            fg = th[0:5] + [lambda: t_fin(0, 16)]
            bgt = th[5:] + [lambda: t_fin(16, 48)]
            return fg, bgt
        th.append(t_fin)
        return th

    def lnv(name, l, which):
        o = VL[name][0] + (l * 2 + which) * 8
        return VEC.ap[:, o:o + 8]

    def derived_A(l):
        rw = dict(r=LA(l) + (VA,), w=(DERL[l].acc(),))
        sh2, sc1, sc2, g1, g2 = (modcol(l, 3, 0, 8), modcol(l, 1, 0, 8), modcol(l, 4, 0, 8),
                                 modcol(l, 2, 0, 8), modcol(l, 5, 0, 8))
        ts(der(l, 'op1', 0, 8), sc1, 1.0, None, ALU.add, **rw)
        ts(der(l, 'op2', 0, 8), sc2, 1.0, None, ALU.add, **rw)
        ob2 = VL['b2'][0] + l * 8
        tt(der(l, 'g2b2', 0, 8), g2, VEC.ap[:, ob2:ob2 + 8], ALU.mult, **rw)
        G, B = lnv('lng', l, 0), lnv('lnb', l, 0)
        tt(der(l, 'ga1', 0, 8), G, der(l, 'op2', 0, 8), ALU.mult, **rw)
        tt(der(l, 'tmp', 0, 8), B, der(l, 'op2', 0, 8), ALU.mult, **rw)
        tt(der(l, 'gb1', 0, 8), der(l, 'tmp', 0, 8), sh2, ALU.add, **rw)
        ts(der(l, 'xg1', 0, 8), G, ALPHA, None, ALU.mult, **rw)
        stt(der(l, 'xb1', 0, 8), B, ALPHA, der(l, 'g2b2', 0, 8), ALU.mult, ALU.add, **rw)
        if l % 3 == 2:
            o = VL['pool_sc'][0]
            tt(der(l, 'z1s', 0, 8), g1, VEC.ap[:, o:o + 8], ALU.mult, **rw)
            ts(der(l, 'hps', 0, 8), der(l, 'op1', 0, 8), 1.0 / ALPHA, None, ALU.mult, **rw)
        else:
            ts(der(l, 'z1s', 0, 8), g1, 1.0, None, ALU.mult, **rw)

    def derived_B(l):
        rw = dict(r=LA(l) + (LA(l + 1) if l + 1 < n_layers else ()) + (VA,), w=(DERL[l].acc(),))
        G, B = lnv('lng', l, 1), lnv('lnb', l, 1)
        if l + 1 < n_layers:
            op1n, sh1n = der(l + 1, 'op1', 0, 8), modcol(l + 1, 0, 0, 8)
            tt(der(l, 'ga2', 0, 8), G, op1n, ALU.mult, **rw)
            tt(der(l, 'tmp', 0, 8), B, op1n, ALU.mult, **rw)
            tt(der(l, 'gb2', 0, 8), der(l, 'tmp', 0, 8), sh1n, ALU.add, **rw)
            ts(der(l, 'xg2', 0, 8), G, ALPHA, None, ALU.mult, **rw)
            ts(der(l, 'xb2', 0, 8), B, ALPHA, None, ALU.mult, **rw)
        else:
            ts(der(l, 'xg2', 0, 8), G, 1.0, None, ALU.mult, **rw)
            ts(der(l, 'xb2', 0, 8), B, 1.0, None, ALU.mult, **rw)

    m_ap = V('m')
    PT = Buf(arena, EXT_LO + 4096, 7 * 256, F32)
    PTAB = Buf(arena, EXT_LO + 6144, 2048, F32)
    act(FRQ.ap, V('jf', 0, 2), AF.Exp, scale=-math.log(10000.0) / 256.0, r=(VA,), w=(FRQ.acc(),))
    for c in range(C):
        part, jj = c // 2, c % 2
        n = 32 if part < 2 else 64
        base = V('posr', 0, 32) if part < 2 else V('posc', 0, 64)
        phase = 0.0 if part % 2 == 0 else math.pi / 2
        t1, qi_f, qf, t2, ng = (PT.ap[:, i * 64:i * 64 + n] for i in range(5))
        qi = PT.ap[:, 64:128].bitcast(I32)[:, 0:n]
        pa = (PT.acc(),)
        ts(t1, base, FRQ.ap[:, jj:jj + 1], phase + math.pi, ALU.mult, ALU.add, r=(VA, FRQ.acc()), w=pa)
        ts(qi, t1, 1.0 / (2 * math.pi), None, ALU.mult, r=pa, w=pa)
        S.op('dve', lambda e, qf=qf, qi=qi: e.tensor_copy(out=qf, in_=qi), pa, pa)
        stt(t2, qf, -2 * math.pi, t1, ALU.mult, ALU.add, r=pa, w=pa)
        ts(ng, t2, 0.0, 2 * math.pi, ALU.is_lt, ALU.mult, r=pa, w=pa)
        tt(t2, t2, ng, ALU.add, r=pa, w=pa)
        ts(t2, t2, 2 * math.pi - 1e-5, 0.0, ALU.min, ALU.max, r=pa, w=pa)
        act(t1, t2, AF.Sin, bias=-math.pi, r=pa, w=pa)
        ts(PTAB.ap[:, c * 64:c * 64 + n], t1, m_ap, None, ALU.mult, r=pa + (VA,), w=(PTAB.acc(),))

    def add_pos(blk):
        c0_, c1_ = blk * TB, (blk + 1) * TB
        for c in range(C):
            part = c // 2
            x3 = X[c].ap[:, c0_:c1_].rearrange("p (a b) -> p a b", b=64)
            if part < 2:
                bc = PTAB.ap[:, c * 64 + blk * 8:c * 64 + blk * 8 + 8].unsqueeze(2).broadcast_to([P, 8, 64])
            else:
                bc = PTAB.ap[:, c * 64:c * 64 + 64].unsqueeze(1).broadcast_to([P, 8, 64])
            tt(x3, x3, bc, ALU.add, r=(X[c].acc(c0_, c1_), PTAB.acc()), w=(X[c].acc(c0_, c1_),))

    add_pos(0)
    fg0, bg0 = mod_thunks(0, startup=True)
    for th in fg0:
        th()
    for blk in range(1, NBLK):
        load_x(blk)
    S.wait('pool', ('d:wmb0', 32))
    S.wait('pool', ('d:wmb1', 32))
    prefetch()
    ts(der(0, 'op1', 0, 8), modcol(0, 1, 0, 8), 1.0, None, ALU.add, r=LA(0), w=(DERL[0].acc(),))
    bg.extend(bg0)
    bg.append(lambda: derived_A(0))

    for blk in range(NBLK):
        c0_, c1_ = blk * TB, (blk + 1) * TB
        if blk > 0:
            add_pos(blk)
        for c in range(C):
            act(H[c].ap[:, c0_:c1_], X[c].ap[:, c0_:c1_], AF.Identity, bias=modcol(0, 0, c), scale=der(0, 'op1', c),
                r=(X[c].acc(c0_, c1_),) + LA(0), w=(H[c].acc(c0_, c1_),))
    for c in range(C):
        ts(X[c].ap, X[c].ap, ALPHA, None, ALU.mult, r=(X[c].acc(),), w=(X[c].acc(),))

    def ln_thunks(l, which, blk, last):
        th = []
        c0, c1 = blk * TB, (blk + 1) * TB
        MEAN, MSQ, RSTD, NMR = LNT
        ga, gb = ('ga1', 'gb1') if which == 0 else ('ga2', 'gb2')
        xg, xb = ('xg1', 'xb1') if which == 0 else ('xg2', 'xb2')
        bm, be = [None], [None]

        for c in range(C):
            def t_z(c=c):
                xa = X[c].ap[:, c0:c1]
                if ZB_ON_DVE:
                    S.op('dve', lambda e, c=c, xa=xa: e.tensor_copy(out=ZB[c].ap, in_=xa), (X[c].acc(c0, c1),), (ZB[c].acc(),))
                else:
                    act(ZB[c].ap, xa, AF.Copy, r=(X[c].acc(c0, c1),), w=(ZB[c].acc(),))
                act(ZS[c].ap, xa, AF.Square, r=(X[c].acc(c0, c1),), w=(ZS[c].acc(),))
            th.append(t_z)

        def t_stats():
            bm[0], be[0] = next_bank(), next_bank()
            mm_group(bm[0], [(ONESM.ap, ZB[c].ap) for c in range(C)], r=[ONESM.acc()] + [ZB[c].acc() for c in range(C)])
            mm_group(be[0], [(ONESM.ap, ZS[c].ap) for c in range(C)], r=[ONESM.acc()] + [ZS[c].acc() for c in range(C)])
            act(MEAN.ap, psb[bm[0]], AF.Copy, r=(('ps', bm[0]),), w=(MEAN.acc(),))
            tt(MSQ.ap, MEAN.ap, MEAN.ap, ALU.mult, r=(MEAN.acc(),), w=(MSQ.acc(),))
            tt(MSQ.ap, psb[be[0]], MSQ.ap, ALU.subtract, r=(('ps', be[0]), MSQ.acc()), w=(MSQ.acc(),))
            ts(MSQ.ap, MSQ.ap, 0.0, LN_EPS, ALU.max, ALU.add, r=(MSQ.acc(),), w=(MSQ.acc(),))
            act(MSQ.ap, MSQ.ap, AF.Ln, r=(MSQ.acc(),), w=(MSQ.acc(),))
        th.append(t_stats)

        def t_stats_b():
            act(RSTD.ap, MSQ.ap, AF.Exp, scale=-0.5, r=(MSQ.acc(),), w=(RSTD.acc(),))
            stt(NMR.ap, MEAN.ap, -1.0, RSTD.ap, ALU.mult, ALU.mult, r=(MEAN.acc(), RSTD.acc()), w=(NMR.acc(),))
        th.append(t_stats_b)

        for c in range(C):
            def t_n(c=c):
                xa = X[c].ap[:, c0:c1]
                xacc = X[c].acc(c0, c1)
                tt(xa, xa, RSTD.ap, ALU.mult, r=(xacc, RSTD.acc()), w=(xacc,))
                tt(xa, xa, NMR.ap, ALU.add, r=(xacc, NMR.acc()), w=(xacc,))
                if not last:
                    act(H[c].ap[:, c0:c1], xa, AF.Identity, bias=der(l, gb, c), scale=der(l, ga, c),
                        r=(xacc,) + LA(l), w=(H[c].acc(c0, c1),))
                if last and which == 1:
                    act(xa, xa, AF.Identity, bias=der(l, xb, c), scale=der(l, xg, c), r=(xacc,) + LA(l), w=(xacc,))
                elif blk == NBLK - 1:
                    xaq.append(lambda c=c: act(X[c].ap, X[c].ap, AF.Identity, bias=der(l, xb, c), scale=der(l, xg, c),
                                               r=(X[c].acc(),) + LA(l), w=(X[c].acc(),)))
                if which == 1 and (not last) and (l + 1) % 3 == 0 and FUSE_A and c < 6:
                    S.dma('sp', lambda e: e.dma_start(out=xsp[c * P:(c + 1) * P, c0:c1], in_=xa), f'xs{c}_{blk}',
                          reads=(xacc,))
                if last and which == 1:
                    out_toks.append(S.dma('sp', lambda e: e.dma_start(out=yT[c * P:(c + 1) * P, c0:c1], in_=xa),
                                          f'yo{c}', reads=(xacc,)))
            th.append(t_n)
        return th

    def z_evac(bank, l, m, blk, scale_ap):
        c0, c1 = blk * TB, (blk + 1) * TB
        xa = X[m].ap[:, c0:c1]
        stt(xa, psb[bank], scale_ap, xa, ALU.mult, ALU.add, r=(('ps', bank), X[m].acc(c0, c1)) + LA(l),
            w=(X[m].acc(c0, c1),))

    def out_proj(l):
        flush_xa()
        if OUTPROJ_BLK:
            ss = [ring_get('k8'), ring_get('k8')]
            for blk in range(NBLK):
                for m in range(C):
                    s_, mi = ss[m // 4], m % 4
                    b = next_bank()
                    mm_group(b, [(RING[s_].ap[:, kc * 512 + mi * P:kc * 512 + (mi + 1) * P],
                                  Y[kc].ap[:, blk * TB:(blk + 1) * TB]) for kc in range(C)],
                             r=[RING[s_].acc()] + [Y[kc].acc(blk * TB, (blk + 1) * TB) for kc in range(C)])
                    z_evac(b, l, m, blk, der(l, 'z1s', m))
                    drip()
                ln_enqueue(l, 0, blk, False)
            ring_release(ss[0])
            ring_release(ss[1])
            return
        s0 = ring_get('k8')
        for mi in range(4):
            for blk in range(NBLK):
                b = next_bank()
                mm_group(b, [(RING[s0].ap[:, kc * 512 + mi * P:kc * 512 + (mi + 1) * P],
                              Y[kc].ap[:, blk * TB:(blk + 1) * TB]) for kc in range(C)],
                         r=[RING[s0].acc()] + [Y[kc].acc(blk * TB, (blk + 1) * TB) for kc in range(C)])
                z_evac(b, l, mi, blk, der(l, 'z1s', mi))
                drip()
        ring_release(s0)
        s1 = ring_get('k8')
        for blk in range(NBLK):
            for mi in range(4):
                b = next_bank()
                mm_group(b, [(RING[s1].ap[:, kc * 512 + mi * P:kc * 512 + (mi + 1) * P],
                              Y[kc].ap[:, blk * TB:(blk + 1) * TB]) for kc in range(C)],
                         r=[RING[s1].acc()] + [Y[kc].acc(blk * TB, (blk + 1) * TB) for kc in range(C)])
                z_evac(b, l, 4 + mi, blk, der(l, 'z1s', 4 + mi))
                drip(LN_DRIP_OUT)
            ln_enqueue(l, 0, blk, False)
        ring_release(s1)

    def lru_mixer(l):
        j = l // 3
        RP = Buf(arena, EXT_LO, 8 * 259 * 4, F32)
        XCB = XCB1
        XC = Buf(arena, RU_LO, 8192, F32)
        A = Buf(arena, RU_LO + 8192, 8192, F32)
        UF = Buf(arena, RU_LO + 16384, 8192, F32)
        UB = Buf(arena, RU_LO + 24576, 8192, F32)
        SQ = XC
        sBD = ring_get('bd')
        BD = RING[sBD]
        lo = j * 64
        cn = LRD.ap[:, lo:lo + 16]
        cn2 = LRD.ap[:, lo + 16:lo + 32]
        la = (LRD.acc(),)
        ol = VL['lru_lam'][0] + j * 16
        act(cn, VEC.ap[:, ol:ol + 16], AF.Exp, scale=-1.0, r=(VA,), w=la)
        act(cn, cn, AF.Ln, bias=1.0, r=la, w=la)
        ts(cn2, cn, -2.0 * LRU_C, None, ALU.mult, r=la, w=la)
        ts(cn, cn, -LRU_C, None, ALU.mult, r=la, w=la)

        fused = FUSE_A and (l > 0 or FUSE_A_L0)
        if fused:
            flush_ln()
            flush()
        gi = [0]
        gts = GT if l == 0 else GT + [Buf(arena, EXT_LO + i * 2048, 2048, F32) for i in range(2)]
        for g2 in (() if fused else range(2)):
          s = ring_get('k8')
          for blk in range(NBLK):
            flush_ln(blk)
            c0, c1 = blk * TB, (blk + 1) * TB
            for cc in range(4):
                c = g2 * 4 + cc
                b = next_bank()
                mm_group(b, [(RING[s].ap[:, kc * 512 + cc * P:kc * 512 + (cc + 1) * P], H[kc].ap[:, c0:c1])
                             for kc in range(C)],
                         r=[RING[s].acc()] + [H[kc].acc(c0, c1) for kc in range(C)])
                g = gts[gi[0] % len(gts)]
                gi[0] += 1
                ga = (g.acc(),)
                pa = (('ps', b),)
                act(g.ap, psb[b], AF.Square, scale=math.sqrt(0.044715), r=pa, w=ga)
                stt(g.ap, g.ap, 1.0, psb[b], ALU.add, ALU.mult, r=ga + pa, w=ga)
                act(g.ap, g.ap, AF.Sigmoid, scale=GELU_K, r=ga, w=ga)
                tt(Y[c].ap[:, c0:c1], g.ap, psb[b], ALU.mult, r=ga + pa, w=(Y[c].acc(c0, c1),))
                drip_bg()
          ring_release(s)
        flush()
        flush_xa()
        NSP = 6
        for c in (range(NSP) if not (fused and l > 0) else ()):
            S.dma('sp', lambda e, c=c: e.dma_start(out=xsp[c * P:(c + 1) * P, :], in_=X[c].ap), f'x{c}',
                  reads=(X[c].acc(),))
        sets = [dict(RP=RP, XCB=XCB, XC=XC, A=A, UF=UF, UB=UB),
                dict(RP=Buf(arena, X_LO, 8 * 259 * 4, F32), XCB=Buf(arena, X_LO + 16896, 4096, BF16),
                     XC=Buf(arena, X_LO + 8704, 8192, F32), A=Buf(arena, X_LO + 8704 + 8192 + 4096, 8192, F32),
                     UF=Buf(arena, X_LO + 8704 + 16384 + 4096, 8192, F32),
                     UB=Buf(arena, X_LO + 8704 + 24576 + 4096, 8192, F32))]
        ocw = VL['lru_cw'][0] + j * 32
        ocb = VL['lru_cb'][0] + j * 8
        oba = VL['lru_ba'][0] + j * 16
        obx = VL['lru_bx'][0] + j * 16
        oh0 = VL['h0'][0] + j * 16

        def chunk_gen(c, st, s, cc, sg=None):
            RP_, XCB_, XC_, A_, UF_, UB_ = st['RP'], st['XCB'], st['XC'], st['A'], st['UF'], st['UB']
            SQ_ = XC_
            rp3 = RP_.ap.rearrange("p (s t) -> p s t", t=259)
            ra = (RP_.acc(),)
            if sg is not None:
                for blk in range(NBLK):
                    c0, c1 = blk * TB, (blk + 1) * TB
                    b = next_bank()
                    mm_group(b, [(RING[sg].ap[:, kc * 512 + cc * P:kc * 512 + (cc + 1) * P], H[kc].ap[:, c0:c1])
                                 for kc in range(C)],
                             r=[RING[sg].acc()] + [H[kc].acc(c0, c1) for kc in range(C)])
                    g_ap = UF_.ap[:, c0:c1]
                    ga = (UF_.acc(c0, c1),)
                    pa = (('ps', b),)
                    act(g_ap, psb[b], AF.Square, scale=math.sqrt(0.044715), r=pa, w=ga)
                    yield
                    stt(g_ap, g_ap, 1.0, psb[b], ALU.add, ALU.mult, r=ga + pa, w=ga)
                    yield
                    act(g_ap, g_ap, AF.Sigmoid, scale=GELU_K, r=ga, w=ga)
                    yield
                    tt(Y[c].ap[:, c0:c1], g_ap, psb[b], ALU.mult, r=ga + pa, w=(Y[c].acc(c0, c1),))
                    yield
            for blk in range(NBLK):
                c0, c1 = blk * TB, (blk + 1) * TB
                b = next_bank()
                mm_group(b, [(RING[s].ap[:, kc * 512 + cc * P:kc * 512 + (cc + 1) * P], H[kc].ap[:, c0:c1])
                             for kc in range(C)],
                         r=[RING[s].acc()] + [H[kc].acc(c0, c1) for kc in range(C)])
                act(rp3[:, 2 * blk:2 * blk + 2, 2:258], psb[b].rearrange("p (s t) -> p s t", t=256), AF.Copy,
                    r=(('ps', b),), w=ra)
                yield
            ts(rp3[:, 1:8, 0:2], rp3[:, 0:7, 256:258], m_ap, None, ALU.mult, r=ra + (VA,), w=ra)
            ts(rp3[:, 0:7, 258:259], rp3[:, 1:8, 2:3], m_ap, None, ALU.mult, r=ra + (VA,), w=ra)
            yield
            xc3 = XC_.ap.rearrange("p (s t) -> p s t", t=256)
            xa_ = (XC_.acc(),)
            cw = [VEC.ap[:, ocw + k * 8 + c:ocw + k * 8 + c + 1] for k in range(4)]
            act(xc3, rp3[:, :, 0:256], AF.Identity, bias=VEC.ap[:, ocb + c:ocb + c + 1], scale=cw[0], r=ra + (VA,), w=xa_)
            yield
            for k in range(1, 4):
                stt(xc3, rp3[:, :, k:k + 256], cw[k], xc3, ALU.mult, ALU.add, r=ra + xa_ + (VA,), w=xa_)
                yield
            act(XCB_.ap, XC_.ap, AF.Copy, r=xa_, w=(XCB_.acc(),))
            yield

            def gate(tp, dst, blk):
                c0, c1 = blk * TB, (blk + 1) * TB
                b = next_bank()
                mm_group(b, [(BD.ap[:, (tp * 8 + c) * P:(tp * 8 + c + 1) * P], XCB_.ap[:, c0:c1])],
                         r=[BD.acc(), XCB_.acc()])
                d = tp // 2
                ob = (oba if tp % 2 == 0 else obx) + d * 8 + c
                act(dst.ap[:, c0:c1], psb[b], AF.Sigmoid, bias=VEC.ap[:, ob:ob + 1], r=(('ps', b), VA),
                    w=(dst.acc(c0, c1),))

            AB_ = Buf(arena, RP_.lo + 8, 8192, F32)
            for blk in range(NBLK):
                gate(1, UF_, blk)
                gate(3, UB_, blk)
                yield
                gate(0, A_, blk)
                gate(2, AB_, blk)
                yield
            assert UB_.lo == UF_.lo + 8192
            u2 = Buf(arena, UF_.lo, 16384, F32)
            u2ap = u2.ap.rearrange("p (d t) -> p d t", t=2048)
            tt(u2ap, u2ap, XC_.ap.unsqueeze(1).broadcast_to([P, 2, 2048]), ALU.mult,
               r=(u2.acc(), XC_.acc()), w=(u2.acc(),))
            yield 'mid'
            sa = (SQ_.acc(),)

            def cn_(d, two):
                o = lo + (16 if two else 0) + d * 8 + c
                return LRD.ap[:, o:o + 1]

            def h0_(d):
                return VEC.ap[:, oh0 + d * 8 + c:oh0 + d * 8 + c + 1]

            def so_(d):
                return ((j * 2 + d) * 8 + c) * 8
            fa_, ba_ = (A_.acc(),), (AB_.acc(),)
            ufa, uba = (UF_.acc(),), (UB_.acc(),)
            act(A_.ap, A_.ap, AF.Exp, scale=cn_(0, False), r=fa_ + la, w=fa_)
            yield
            act(AB_.ap, AB_.ap, AF.Exp, scale=cn_(1, False), r=ba_ + la, w=ba_)
            yield
            act(SQ_.ap, A_.ap, AF.Square, r=fa_, w=sa)
            yield
            ts(SQ_.ap, SQ_.ap, 1.0, None, ALU.min, r=sa, w=sa)
            yield
            act(SQ_.ap, SQ_.ap, AF.Sqrt, bias=1.0, scale=-1.0, r=sa, w=sa)
            yield
            tt(UF_.ap, UF_.ap, SQ_.ap, ALU.mult, r=ufa + sa, w=ufa)
            yield
            act(SQ_.ap, AB_.ap, AF.Square, r=ba_, w=sa)
            yield
            ts(A_.ap[:, 256:2048:256], A_.ap[:, 256:2048:256], m_ap, None, ALU.mult, r=fa_ + (VA,), w=fa_)
            S.op('dve', lambda e: e.tensor_tensor_scan(
                out=UF_.ap, data0=A_.ap, data1=UF_.ap, initial=h0_(0), op0=ALU.mult, op1=ALU.add),
                fa_ + ufa + (VA,), ufa)
            yield
            ts(SQ_.ap, SQ_.ap, 1.0, None, ALU.min, r=sa, w=sa)
            yield
            act(SQ_.ap, SQ_.ap, AF.Sqrt, bias=1.0, scale=-1.0, r=sa, w=sa)
            yield
            S.op('dve', lambda e: e.tensor_copy(out=STB.ap[:, so_(0):so_(0) + 8], in_=UF_.ap[:, 255:2048:256]),
                 ufa, (STB.acc(so_(0), so_(0) + 8),))
            tt(UB_.ap, UB_.ap, SQ_.ap, ALU.mult, r=uba + sa, w=uba)
            yield
            ts(AB_.ap[:, 255:1792:256], AB_.ap[:, 255:1792:256], m_ap, None, ALU.mult, r=ba_ + (VA,), w=ba_)
            S.op('dve', lambda e: e.tensor_tensor_scan(
                out=UB_.ap[:, ::-1], data0=AB_.ap[:, ::-1], data1=UB_.ap[:, ::-1], initial=h0_(1),
                op0=ALU.mult, op1=ALU.add), ba_ + uba + (VA,), uba)
            yield
            S.op('dve', lambda e: e.tensor_copy(out=STB.ap[:, so_(1):so_(1) + 8], in_=UB_.ap[:, 0:2048:256]),
                 uba, (STB.acc(so_(1), so_(1) + 8),))
            yield
            tt(UF_.ap, UF_.ap, UB_.ap, ALU.add, r=ufa + uba, w=ufa)
            yield
            tt(Y[c].ap, Y[c].ap, UF_.ap, ALU.mult, r=(Y[c].acc(), UF_.acc()), w=(Y[c].acc(),))
            yield

        for st_ in sets:
            r3_ = st_['RP'].ap.rearrange("p (s t) -> p s t", t=259)
            S.op('dve', lambda e, r3_=r3_: e.memset(r3_[:, 0:1, 0:2], 0.0), (), (st_['RP'].acc(),))
            S.op('dve', lambda e, r3_=r3_: e.memset(r3_[:, 7:8, 258:259], 0.0), (), (st_['RP'].acc(),))
        if fused:
            gslots = [ring_get('k8'), None]
            slots = [ring_get('k8'), None]
        else:
            gslots = [None, None]
            slots = [ring_get('k8'), ring_get('k8')]
        pending = deque(range(C))
        active = []
        can_start = [True]
        while pending or active:
            if len(active) < 2 and pending and (can_start[0] or not active):
                c = pending.popleft()
                if fused and c == 4:
                    gslots[1] = ring_get('k8')
                    slots[1] = ring_get('k8')
                active.append((c, chunk_gen(c, sets[(c + 1) % 2], slots[c // 4], c % 4, gslots[c // 4])))
                can_start[0] = False
            for item in list(active):
                try:
                    if (next(item[1]) == 'mid' or not LRU_SKEW) and item is active[-1]:
                        can_start[0] = True
                        if fused and item[0] == 3:
                            ring_release(gslots[0])
                            ring_release(slots[0])
                except StopIteration:
                    active.remove(item)
                    if item[0] == 3 and not fused:
                        ring_release(slots[0])
                    if item[0] == 7:
                        ring_release(slots[1])
                        if fused:
                            ring_release(gslots[1])
                    if item[0] == 6:
                        for c2 in range(NSP):
                            S.dma('sp', lambda e, c2=c2: e.dma_start(out=X[c2].ap, in_=xsp[c2 * P:(c2 + 1) * P, :]),
                                  f'x{c2}', writes=(X[c2].acc(),))
        ring_release(sBD)
        out_proj(l)

    def sconv_mixer(l):
        BF_ = Buf(arena, RU_LO, 8192, F32)
        TT_ = Buf(arena, RU_LO + 8192, 8192, F32)
        VT = Buf(arena, RU_LO + 16384, 2048, F32)
        CV = Buf(arena, EXT_LO, 8 * 258 * 4, F32)
        cv3 = CV.ap.rearrange("p (s t) -> p s t", t=258)
        flush_ln()
        flush()
        S.op('dve', lambda e: e.memset(CV.ap, 0.0), (), (CV.acc(),))
        ocw = VL['sc_cw'][0]
        for c in range(C):
            s = ring_get('sc')
            for blk in range(NBLK):
                c0, c1 = blk * TB, (blk + 1) * TB
                bks = []
                for part in range(3):
                    b = next_bank()
                    bks.append(b)
                    mm_group(b, [(RING[s].ap[:, kc * 384 + part * P:kc * 384 + (part + 1) * P], H[kc].ap[:, c0:c1])
                                 for kc in range(C)],
                             r=[RING[s].acc()] + [H[kc].acc(c0, c1) for kc in range(C)])
                act(BF_.ap[:, c0:c1], psb[bks[0]], AF.Copy, r=(('ps', bks[0]),), w=(BF_.acc(c0, c1),))
                act(VT.ap, psb[bks[2]], AF.Copy, r=(('ps', bks[2]),), w=(VT.acc(),))
                tt(cv3[:, 2 * blk:2 * blk + 2, 1:257], psb[bks[1]].rearrange("p (s t) -> p s t", t=256),
                   VT.ap.rearrange("p (s t) -> p s t", t=256), ALU.mult, r=(('ps', bks[1]), VT.acc()), w=(CV.acc(),))
            ring_release(s)
            ca = (CV.acc(),)
            ts(cv3[:, 1:8, 0:1], cv3[:, 0:7, 256:257], m_ap, None, ALU.mult, r=ca + (VA,), w=ca)
            ts(cv3[:, 0:7, 257:258], cv3[:, 1:8, 1:2], m_ap, None, ALU.mult, r=ca + (VA,), w=ca)
            t3 = TT_.ap.rearrange("p (s t) -> p s t", t=256)
            ta = (TT_.acc(),)
            cw = [VEC.ap[:, ocw + k * 8 + c:ocw + k * 8 + c + 1] for k in range(3)]
            ts(t3, cv3[:, :, 0:256], cw[0], None, ALU.mult, r=ca + (VA,), w=ta)
            for k in (1, 2):
                stt(t3, cv3[:, :, k:k + 256], cw[k], t3, ALU.mult, ALU.add, r=ca + ta + (VA,), w=ta)
            tt(Y[c].ap, BF_.ap, TT_.ap, ALU.mult, r=(BF_.acc(), TT_.acc()), w=(Y[c].acc(),))
        out_proj(l)

    def pool_mixer(l):
        PW = 272
        NF = 8 * PW
        psets = [tuple(Buf(arena, base + i * 8704, 8 * PW * 4, F32) for i in range(3)) for base in (RU_LO, H_LO)]
        flush_ln()
        flush_xa()
        flush()
        for ps_ in psets:
            S.op('dve', lambda e, hp=ps_[0]: e.memset(hp.ap, 0.0), (), (ps_[0].acc(),))
        oF = VL['poolF'][0]
        fa = (FM.acc(),)
        ts(FM.ap[:, 64:65], m_ap, -1.0, 1.0, ALU.mult, ALU.add, r=(VA,), w=fa)
        ts(FM.ap[:, 0:64], VEC.ap[:, oF:oF + 64], FM.ap[:, 64:65], m_ap, ALU.mult, ALU.add, r=(VA,) + fa, w=fa)
        sW = ring_get('pool')

        def pool_gen(c, HP, PA, PB):
            hp3 = HP.ap.rearrange("p (s t) -> p s t", t=PW)
            ha, paa, pba = (HP.acc(),), (PA.acc(),), (PB.acc(),)
            g = c // 2
            w = POOL_WINDOWS[g]
            act(hp3[:, :, 8:264], X[c].ap.rearrange("p (s t) -> p s t", t=256), AF.Identity,
                bias=modcol(l, 0, c), scale=der(l, 'hps', c), r=(X[c].acc(),) + LA(l), w=ha)
            yield
            ts(hp3[:, 1:8, 0:8], hp3[:, 0:7, 256:264], m_ap, None, ALU.mult, r=ha + (VA,), w=ha)
            ts(hp3[:, 0:7, 264:272], hp3[:, 1:8, 8:16], m_ap, None, ALU.mult, r=ha + (VA,), w=ha)
            yield
            tt(PA.ap[:, 1:NF], HP.ap[:, 0:NF - 1], HP.ap[:, 1:NF], ALU.add, r=ha, w=paa)
            yield
            cur, cura, oth, otha = PA, paa, PB, pba
            vlo, vhi = 1, NF
            lvl = 2
            while lvl < w:
                hs = lvl // 2
                nlo, nhi = vlo + hs, vhi - hs
                tt(oth.ap[:, nlo:nhi], cur.ap[:, nlo - hs:nhi - hs], cur.ap[:, nlo + hs:nhi + hs], ALU.add,
                   r=cura, w=otha)
                yield
                cur, cura, oth, otha = oth, otha, cur, cura
                vlo, vhi = nlo, nhi
                lvl *= 2
            s3 = cur.ap.rearrange("p (s t) -> p s t", t=PW)
            hw = w // 2
            fo = oF + (g * 2) * 8
            FLg = VEC.ap[:, fo:fo + hw]
            FRg = VEC.ap[:, fo + 8:fo + 8 + hw - 1] if hw > 1 else None
            FmL = FM.ap[:, (g * 2) * 8:(g * 2) * 8 + hw]
            FmR = FM.ap[:, (g * 2 + 1) * 8:(g * 2 + 1) * 8 + hw - 1] if hw > 1 else None
            tt(s3[:, 0:1, 8:8 + hw], s3[:, 0:1, 8:8 + hw], FLg.unsqueeze(1), ALU.mult, r=cura + (VA,), w=cura)
            tt(s3[:, 1:8, 8:8 + hw], s3[:, 1:8, 8:8 + hw], FmL.unsqueeze(1).broadcast_to([P, 7, hw]), ALU.mult,
               r=cura + fa, w=cura)
            if hw > 1:
                e0 = 8 + 256 - (hw - 1)
                tt(s3[:, 7:8, e0:264], s3[:, 7:8, e0:264], FRg.unsqueeze(1), ALU.mult, r=cura + (VA,), w=cura)
                tt(s3[:, 0:7, e0:264], s3[:, 0:7, e0:264], FmR.unsqueeze(1).broadcast_to([P, 7, hw - 1]), ALU.mult,
                   r=cura + fa, w=cura)
            yield
            stt(Y[c].ap.rearrange("p (s t) -> p s t", t=256), s3[:, :, 8:264], 1.0 / w, hp3[:, :, 8:264],
                ALU.mult, ALU.subtract, r=cura + ha, w=(Y[c].acc(),))
            yield

        pend = deque(range(C))
        actv = []
        while pend or actv:
            while len(actv) < 2 and pend:
                c = pend.popleft()
                actv.append(pool_gen(c, *psets[c % 2]))
            for gnr in list(actv):
                try:
                    next(gnr)
                except StopIteration:
                    actv.remove(gnr)
        for blk in range(NBLK):
            c0, c1 = blk * TB, (blk + 1) * TB
            for m in range(C):
                g, mi = m // 2, m % 2
                b = next_bank()
                mm_group(b, [(RING[sW].ap[:, (g * 2 + kc) * 256 + mi * P:(g * 2 + kc) * 256 + (mi + 1) * P],
                              Y[g * 2 + kc].ap[:, c0:c1]) for kc in range(2)],
                         r=[RING[sW].acc()] + [Y[g * 2 + kc].acc(c0, c1) for kc in range(2)])
                z_evac(b, l, m, blk, der(l, 'z1s', m))
                drip()
            ln_enqueue(l, 0, blk, False)
        ring_release(sW)

    def mlp(l):
        last = (l == n_layers - 1)
        flush()
        if not last:
            bg.extend(mod_thunks(l + 1))
        ob1 = VL['b1'][0] + l * 32
        ri = [0]
        un = [0]

        def drip4():
            un[0] += 1
            if un[0] % 4 == 0:
                drip_bg()

        def w1_unit(s, fq, hf, jj, blk):
            jl = hf * 4 + jj
            j = fq * 8 + jl
            c0, c1 = blk * TB, (blk + 1) * TB
            b = next_bank()
            mm_group(b, [(RING[s].ap[:, kc * 512 + jj * P:kc * 512 + (jj + 1) * P], H[kc].ap[:, c0:c1])
                         for kc in range(C)],
                     r=[RING[s].acc()] + [H[kc].acc(c0, c1) for kc in range(C)])
            rt = RT[ri[0] % 4]
            ri[0] += 1
            act(rt.ap, psb[b], AF.Relu, bias=VEC.ap[:, ob1 + j:ob1 + j + 1], r=(('ps', b), VA), w=(rt.acc(),))
            tt(HID[jl].ap[:, c0:c1], rt.ap, rt.ap, ALU.mult, r=(rt.acc(),), w=(HID[jl].acc(c0, c1),))
            drip4()

        for fq in range(4):
            if fq == 0 and not FQ0_BLK:
                flush_ln()
            if fq == 0 and FQ0_BLK:
                ss = [ring_get('k8'), ring_get('k8')]
                for blk in range(NBLK):
                    flush_ln(blk)
                    for hf in range(2):
                        for jj in range(4):
                            w1_unit(ss[hf], fq, hf, jj, blk)
                ring_release(ss[0])
                ring_release(ss[1])
            for hf in (range(2) if (fq > 0 or not FQ0_BLK) else ()):
                s = ring_get('k8')
                for jj in range(4):
                    jl = hf * 4 + jj
                    j = fq * 8 + jl
                    for blk in range(NBLK):
                        c0, c1 = blk * TB, (blk + 1) * TB
                        b = next_bank()
                        mm_group(b, [(RING[s].ap[:, kc * 512 + jj * P:kc * 512 + (jj + 1) * P], H[kc].ap[:, c0:c1])
                                     for kc in range(C)],
                                 r=[RING[s].acc()] + [H[kc].acc(c0, c1) for kc in range(C)])
                        rt = RT[ri[0] % 4]
                        ri[0] += 1
                        act(rt.ap, psb[b], AF.Relu, bias=VEC.ap[:, ob1 + j:ob1 + j + 1], r=(('ps', b), VA), w=(rt.acc(),))
                        tt(HID[jl].ap[:, c0:c1], rt.ap, rt.ap, ALU.mult, r=(rt.acc(),), w=(HID[jl].acc(c0, c1),))
                        drip4()
                ring_release(s)
            flush_xa()
            if fq == 3:
                flush()
                if not last:
                    derived_A(l + 1)
                derived_B(l)
                s0 = ring_get('k8')
                s1 = ring_get('k8')
                for blk in range(NBLK):
                    c0, c1 = blk * TB, (blk + 1) * TB
                    for m in range(C):
                        s, mi = (s0, m) if m < 4 else (s1, m - 4)
                        b = next_bank()
                        mm_group(b, [(RING[s].ap[:, kc * 512 + mi * P:kc * 512 + (mi + 1) * P], HID[kc].ap[:, c0:c1])
                                     for kc in range(C)],
                                 r=[RING[s].acc()] + [HID[kc].acc(c0, c1) for kc in range(C)])
                        z_evac(b, l, m, blk, modcol(l, 5, m))
                        drip(LN_DRIP_W2)
                    ln_enqueue(l, 1, blk, last)
                ring_release(s0)
                ring_release(s1)
            else:
                for mq in range(2):
                    s = ring_get('k8')
                    for mi in range(4):
                        m = mq * 4 + mi
                        for blk in range(NBLK):
                            c0, c1 = blk * TB, (blk + 1) * TB
                            b = next_bank()
                            mm_group(b, [(RING[s].ap[:, kc * 512 + mi * P:kc * 512 + (mi + 1) * P], HID[kc].ap[:, c0:c1])
                                         for kc in range(C)],
                                     r=[RING[s].acc()] + [HID[kc].acc(c0, c1) for kc in range(C)])
                            z_evac(b, l, m, blk, modcol(l, 5, m))
                            drip4()
                    ring_release(s)
        flush()
        if last:
            flush_ln()

    for l in range(n_layers):
        kind = l % 3
        if kind == 0:
            lru_mixer(l)
        elif kind == 1:
            sconv_mixer(l)
        else:
            pool_mixer(l)
        mlp(l)

    flush_xa()
    toks = {}
    for tk in out_toks:
        toks[tk[0]] = max(toks.get(tk[0], 0), tk[1])
    tk = S.dma('sp', lambda e: e.dma_start(out=st_d[:, :], in_=STB.ap), 'vecs', reads=(STB.acc(),))
    toks[tk[0]] = tk[1]
    for k, v in toks.items():
        S.wait('sp', (k, v))

    semnames = list(ENGS) + sorted({it[2] for e in ENGS for it in S.prog[e] if it[0] == 'd'})
    sems = {n: es.enter_context(nc.semaphore(n.replace(':', '_'))) for n in semnames}

    def run(name, e):
        for it in S.prog[name]:
            if it[0] == 'w':
                e.wait_ge(sems[it[1]], it[2])
            elif it[0] == 'o':
                it[1](e).then_inc(sems[name], 1)
            else:
                it[1](e).then_inc(sems[it[2]], 16)

    with nc.Block() as block:
        @block.tensor
        def _(e):
            run('pe', e)

        @block.scalar
        def _(e):
            run('act', e)

        @block.vector
        def _(e):
            run('dve', e)

        @block.gpsimd
        def _(e):
            run('pool', e)

        @block.sync
        def _(e):
            run('sp', e)
    es.close()
    return nc, plan


def make_inputs(inp, n_layers=DEPTH):
    plan = weight_plan(n_layers)
    wstream = np.concatenate([pack_group(g, inp) for g in plan], axis=0)
    wmodT = np.ascontiguousarray(np.transpose(inp['w_mod'], (0, 2, 1))).reshape(4 * 6144, D)
    onesm = np.full((P, P), 1.0 / D, np.float32)

    base = np.zeros((P, NV), np.float32)

    def put(name, arr):
        o, n = VL[name]
        base[:, o:o + n] = np.asarray(arr, np.float32).reshape(P, n)

    put('bmod', fm(inp['b_mod']))
    put('lng', fm(inp['ln_g']))
    put('lnb', fm(inp['ln_b']))
    put('b1', fm(inp['mlp_b1']))
    put('b2', fm(inp['mlp_b2']))
    put('lru_cw', fm(inp['lru_conv_w']))
    put('lru_cb', fm(inp['lru_conv_b']))
    put('lru_ba', fm(inp['lru_b_a']))
    put('lru_bx', fm(inp['lru_b_x']))
    put('lru_lam', fm(inp['lru_lambda']))
    put('sc_cw', fm(inp['sc_conv_w'][0]))
    put('pool_sc', fm(inp['pool_scale'][0]))
    jf = np.stack([np.arange(P), np.arange(P) + P], axis=1).astype(np.float32)
    put('jf', jf)
    put('posr', np.broadcast_to(np.arange(32, dtype=np.float32), (P, 32)))
    put('posc', np.broadcast_to(np.arange(64, dtype=np.float32), (P, 64)))
    F = np.ones((4, 2, 8), np.float32)
    for g, w in enumerate(POOL_WINDOWS):
        hw = w // 2
        for t in range(hw):
            F[g, 0, t] = w / float(t + hw)
        for i in range(hw - 1):
            t = 256 - (hw - 1) + i
            F[g, 1, i] = w / float(256 - t + hw)
    put('poolF', np.broadcast_to(F.reshape(1, 64), (P, 64)))

    in_maps = []
    for core in range(NCORES):
        v = base.copy()
        xT = np.zeros((D, T), np.float32)
        if core in SAMPLE_CORES:
            b = SAMPLE_CORES[core]
            xT[:, :] = inp['x_sample'][b].T
            o, n = VL['h0']
            v[:, o:o + n] = fm(inp['state_lru'][b]).reshape(P, n)
            v[:, VL['m'][0]] = 1.0
            cond = inp['c'][b]
        elif core in PROMPT_CORES:
            for si, sq in enumerate(PROMPT_CORES[core]):
                xT[:, si * L:(si + 1) * L] = inp['x_prompt'][sq].T
            cond = inp['c_ctx']
        else:
            v[:] = 0.0
            cond = np.zeros((D,), np.float32)
        in_maps.append({
            "xT": xT,
            "cvec": np.ascontiguousarray(np.broadcast_to(np.asarray(cond, np.float32), (P, D))),
            "vecs": v,
            "wmodT": wmodT,
            "wstream": wstream,
            "onesm": onesm,
        })
    return in_maps


_CACHE = {}


def run_device(inp, n_layers=DEPTH, trace=False):
    if n_layers not in _CACHE:
        _CACHE[n_layers] = build(n_layers)
    nc, _ = _CACHE[n_layers]
    in_maps = make_inputs(inp, n_layers)
    kw = dict(trace=True) if trace else {}
    return run_bass_kernel_spmd(nc, in_maps, core_ids=list(range(NCORES)), **kw)


def assemble(res):
    B, SEQ = 16, 256
    y_prompt = np.zeros((B, SEQ, D), np.float32)
    y_sample = np.zeros((2, T, D), np.float32)
    new_state = np.zeros((B, 2, 2, D), np.float32)
    for core in range(NCORES):
        r = res.results[core]
        y = np.asarray(r["yT"]).T
        if core in SAMPLE_CORES:
            y_sample[SAMPLE_CORES[core]] = y
        elif core in PROMPT_CORES:
            st = np.asarray(r["st"]).reshape(P, 2, 2, 8, 8)
            for si, sq in enumerate(PROMPT_CORES[core]):
                y_prompt[sq] = y[si * L:(si + 1) * L]
                new_state[sq] = np.transpose(st[:, :, :, :, si], (1, 2, 3, 0)).reshape(2, 2, D)
    return y_prompt, y_sample, new_state


def kernel(**inputs):
    inp = {k: np.asarray(v) for k, v in inputs.items()}
    res = run_device(inp)
    return assemble(res)
```
